# Optimizing a Trainium2 kernel written in Bass

```python
import math
import jax, jax.numpy as jnp
from jax import lax
import numpy as np

D_MODEL = 1024
BATCH = 4
SEQ = 8192
DEPTH = 4

GRID_W = 64
FN_GROUPS = 4
FN_GROUP_DIM = 64
FN_WIDTH = FN_GROUPS * FN_GROUP_DIM
NA_HEADS = 8
NA_HEAD_DIM = 64
NA_WIDTH = NA_HEADS * NA_HEAD_DIM
NA_KR_MAX = 8
NA_KC = 16
NA_QC = 16
NA_KCB = 2 * NA_KC
NA_NCB = GRID_W // NA_QC
NEG_INF = -1e30
SSM_GROUPS = 16
SSM_GROUP_DIM = 16
SSM_WIDTH = SSM_GROUPS * SSM_GROUP_DIM
SSM_STATE = 64
DT_MIN = 1e-3
DT_MAX = 1e-1
N_BRANCH = 3
D_FF = 4 * D_MODEL
D_IN = FN_WIDTH + 3 * NA_WIDTH + SSM_WIDTH + N_BRANCH * D_MODEL
RMS_EPS = 1e-6

kernel_name = 'hybrid_fnet_natten_s5_encoder'


def rms_norm(x, g):
    xf = x.astype(jnp.float32)
    y = xf * lax.rsqrt(jnp.mean(xf * xf, axis=-1, keepdims=True) + RMS_EPS)
    return (y * g.astype(jnp.float32)).astype(x.dtype)


def fourier_mix(u):
    b, s, _ = u.shape
    ug = u.astype(jnp.float32).reshape(b, s, FN_GROUPS, FN_GROUP_DIM)
    f = jnp.fft.fftn(ug, axes=(1, 3), norm='ortho')
    return jnp.real(f).reshape(b, s, FN_WIDTH).astype(u.dtype)


def _window_starts(n_pos, n_win):
    pos = np.arange(n_pos)
    return np.clip(pos - n_win // 2, 0, n_pos - n_win)


def neighbourhood_attention(q, k, v, rpb):
    b, s, _ = q.shape
    rows = s // GRID_W
    kr = min(NA_KR_MAX, rows)

    def grid(t):
        return t.reshape(b, rows, GRID_W, NA_HEADS, NA_HEAD_DIM).transpose(0, 3, 1, 2, 4)

    qg, kg, vg = grid(q), grid(k), grid(v)
    key_rows = _window_starts(rows, kr)[:, None] + np.arange(kr)
    qcol = np.arange(GRID_W).reshape(NA_NCB, NA_QC)
    blk_start = np.clip(np.arange(NA_NCB) * NA_QC - NA_KC // 2, 0, GRID_W - NA_KCB)
    key_cols = blk_start[:, None] + np.arange(NA_KCB)
    win_start = _window_starts(GRID_W, NA_KC)[qcol]
    in_win = ((key_cols[:, None, :] >= win_start[:, :, None])
              & (key_cols[:, None, :] < win_start[:, :, None] + NA_KC))

    ri = key_rows[:, None, :, None]
    ci = key_cols[None, :, None, :]
    kb = kg[:, :, ri, ci]
    vb = vg[:, :, ri, ci]
    qb = qg.reshape(b, NA_HEADS, rows, NA_NCB, NA_QC, NA_HEAD_DIM)

    scores = jnp.einsum('bhrjqd,bhrjkcd->bhrjqkc', qb, kb).astype(jnp.float32)
    dr = key_rows - np.arange(rows)[:, None] + NA_KR_MAX - 1
    dc = np.clip(key_cols[:, None, :] - qcol[:, :, None], -(NA_KC - 1), NA_KC - 1) + NA_KC - 1
    bias = rpb[:, dr[:, None, None, :, None], dc[None, :, :, None, :]].astype(jnp.float32)
    scores = jnp.where(in_win[:, :, None, :], scores + bias, NEG_INF)
    p = jax.nn.softmax(scores, axis=(-2, -1)).astype(vb.dtype)
    o = jnp.einsum('bhrjqkc,bhrjkcd->bhrjqd', p, vb)
    o = o.reshape(b, NA_HEADS, rows, GRID_W, NA_HEAD_DIM).transpose(0, 2, 3, 1, 4)
    return o.reshape(b, s, NA_WIDTH)


def _ssm_scan(ug, a_re, a_im, log_dt, b_re, b_im, c_re, c_im, reverse):
    lam = lax.complex(a_re.astype(jnp.float32), a_im.astype(jnp.float32))
    dt = jnp.exp(log_dt.astype(jnp.float32))[:, None]
    lam_bar = jnp.exp(lam * dt)
    b_bar = ((lam_bar - 1.0) / lam)[:, :, None] * lax.complex(
        b_re.astype(jnp.float32), b_im.astype(jnp.float32))
    bu = jnp.einsum('bsgc,gpc->bsgp', ug.astype(jnp.complex64), b_bar)
    a = jnp.broadcast_to(lam_bar, bu.shape)

    def combine(left, right):
        a_l, x_l = left
        a_r, x_r = right
        return a_l * a_r, a_r * x_l + x_r

    _, xs = lax.associative_scan(combine, (a, bu), axis=1, reverse=reverse)
    c = lax.complex(c_re.astype(jnp.float32), c_im.astype(jnp.float32))
    return jnp.real(jnp.einsum('bsgp,gcp->bsgc', xs, c))


def ssm_branch(u, a_re, a_im, log_dt, b_re, b_im, c_re, c_im, d_skip, w_glu):
    b, s, _ = u.shape
    uf = u.astype(jnp.float32)
    ug = uf.reshape(b, s, SSM_GROUPS, SSM_GROUP_DIM)
    y = d_skip.astype(jnp.float32) * uf
    for direction in range(2):
        y = y + _ssm_scan(ug, a_re[direction], a_im[direction], log_dt[direction],
                          b_re[direction], b_im[direction], c_re[direction], c_im[direction],
                          reverse=(direction == 1)).reshape(b, s, SSM_WIDTH)
    y = jax.nn.gelu(y)
    y = y * jax.nn.sigmoid(y @ w_glu.astype(jnp.float32))
    return y.astype(u.dtype)


def setup_inputs(seed: int = 0) -> dict:
    key = jax.random.key(seed)
    ks = jax.random.split(key, 24)
    L = DEPTH

    def nrm(i, shape, scale):
        return scale * jax.random.normal(ks[i], shape, jnp.float32)

    n_idx = jnp.arange(SSM_STATE, dtype=jnp.float32)
    shp_a = (L, 2, SSM_GROUPS, SSM_STATE)
    return {
        'x': nrm(0, (BATCH, SEQ, D_MODEL), 1.0),
        'g_mix': 1.0 + nrm(1, (L, D_MODEL), 0.01),
        'w_in': nrm(2, (L, D_MODEL, D_IN), D_MODEL ** -0.5),
        'na_rpb': nrm(3, (L, NA_HEADS, 2 * NA_KR_MAX - 1, 2 * NA_KC - 1), 0.02),
        'ssm_a_re': -0.5 + nrm(4, shp_a, 0.01),
        'ssm_a_im': math.pi * n_idx + nrm(5, shp_a, 0.01),
        'ssm_log_dt': jax.random.uniform(ks[6], (L, 2, SSM_GROUPS), jnp.float32,
                                         math.log(DT_MIN), math.log(DT_MAX)),
        'ssm_b_re': nrm(7, (L, 2, SSM_GROUPS, SSM_STATE, SSM_GROUP_DIM), (2 * SSM_GROUP_DIM) ** -0.5),
        'ssm_b_im': nrm(8, (L, 2, SSM_GROUPS, SSM_STATE, SSM_GROUP_DIM), (2 * SSM_GROUP_DIM) ** -0.5),
        'ssm_c_re': nrm(9, (L, 2, SSM_GROUPS, SSM_GROUP_DIM, SSM_STATE), 2 ** -0.5),
        'ssm_c_im': nrm(10, (L, 2, SSM_GROUPS, SSM_GROUP_DIM, SSM_STATE), 2 ** -0.5),
        'ssm_d': nrm(11, (L, SSM_WIDTH), 1.0),
        'w_glu': nrm(12, (L, SSM_WIDTH, SSM_WIDTH), SSM_WIDTH ** -0.5),
        'w_br_fn': nrm(13, (L, FN_WIDTH, D_MODEL), FN_WIDTH ** -0.5),
        'w_br_na': nrm(14, (L, NA_WIDTH, D_MODEL), NA_WIDTH ** -0.5),
        'w_br_ssm': nrm(15, (L, SSM_WIDTH, D_MODEL), SSM_WIDTH ** -0.5),
        'w_out': nrm(16, (L, D_MODEL, D_MODEL), D_MODEL ** -0.5),
        'g_ffn': 1.0 + nrm(17, (L, D_MODEL), 0.01),
        'w_up': nrm(18, (L, D_MODEL, D_FF), D_MODEL ** -0.5),
        'w_down': nrm(19, (L, D_FF, D_MODEL), D_FF ** -0.5),
        'g_final': 1.0 + nrm(20, (D_MODEL,), 0.01),
    }


def reference(x, g_mix, w_in, na_rpb, ssm_a_re, ssm_a_im, ssm_log_dt, ssm_b_re, ssm_b_im,
              ssm_c_re, ssm_c_im, ssm_d, w_glu, w_br_fn, w_br_na, w_br_ssm, w_out,
              g_ffn, w_up, w_down, g_final):
    b, s, _ = x.shape
    splits = [FN_WIDTH, FN_WIDTH + NA_WIDTH, FN_WIDTH + 2 * NA_WIDTH,
              FN_WIDTH + 3 * NA_WIDTH, FN_WIDTH + 3 * NA_WIDTH + SSM_WIDTH]
    q_scale = NA_HEAD_DIM ** -0.5
    for l in range(DEPTH):
        h = rms_norm(x, g_mix[l])
        z = h @ w_in[l]
        u_fn, q, k, v, u_ssm, gate_logits = jnp.split(z, splits, axis=-1)
        gates = jax.nn.sigmoid(gate_logits.reshape(b, s, N_BRANCH, D_MODEL))
        y_fn = fourier_mix(u_fn) @ w_br_fn[l]
        y_na = neighbourhood_attention(q * q_scale, k, v, na_rpb[l]) @ w_br_na[l]
        y_ssm = ssm_branch(u_ssm, ssm_a_re[l], ssm_a_im[l], ssm_log_dt[l], ssm_b_re[l], ssm_b_im[l],
                           ssm_c_re[l], ssm_c_im[l], ssm_d[l], w_glu[l]) @ w_br_ssm[l]
        merged = gates[:, :, 0] * y_fn + gates[:, :, 1] * y_na + gates[:, :, 2] * y_ssm
        x = x + merged @ w_out[l]
        h = rms_norm(x, g_ffn[l])
        x = x + jnp.square(jax.nn.relu(h @ w_up[l])) @ w_down[l]
    return rms_norm(x, g_final)
```

```python
import numpy as np
import contextlib
import concourse.bass as bass
import concourse.mybir as mybir


F32 = mybir.dt.float32
BF16 = mybir.dt.bfloat16
AF = mybir.ActivationFunctionType
ALU = mybir.AluOpType
AX = mybir.AxisListType


class Buf:
    __slots__ = ("name", "w", "r")

    def __init__(self, name=""):
        self.name = name
        self.w = None
        self.r = []


class Eng:
    def __init__(self, ctx, eng, name, is_pe=False):
        self.ctx = ctx
        self.eng = eng
        self.name = name
        self.is_pe = is_pe
        self.sem = ctx.nc.alloc_semaphore(name="s_" + name)
        self.count = 0
        self.seen = {}
        self.dsems = None
        self.dcounts = None
        self.dnext = 0

    def _wait(self, tok):
        key, sem, val = tok
        if self.seen.get(key, 0) >= val:
            return
        self.eng.wait_ge(sem, val)
        self.seen[key] = val

    def _deps(self, reads, writes):
        toks = []
        for b in reads:
            if b.w is not None:
                toks.append(b.w)
        for b in writes:
            if b.w is not None:
                toks.append(b.w)
            toks.extend(b.r)
        return toks

    def op(self, fn, reads=(), writes=()):
        for tok in self.ctx.barrier_toks:
            if not (self.is_pe and tok[0] == self.name):
                self._wait(tok)
        for tok in self._deps(reads, writes):
            if self.is_pe and tok[0] == self.name:
                continue
            self._wait(tok)
        inst = fn(self.eng)
        self.count += 1
        inst.then_inc(self.sem, 1)
        tok = (self.name, self.sem, self.count)
        self.seen[self.name] = max(self.seen.get(self.name, 0), 0)
        for b in writes:
            b.w = tok
            b.r = []
        for b in reads:
            if b not in writes:
                b.r.append(tok)
                if len(b.r) > 12:
                    last = {}
                    for t in b.r:
                        if t[0] not in last or last[t[0]][2] < t[2]:
                            last[t[0]] = t
                    b.r = list(last.values())
        return tok

    def dma(self, out, in_, reads=(), writes=(), nsem=8, **kw):
        if self.dsems is None:
            self.dsems = [self.ctx.nc.alloc_semaphore(name=f"d_{self.name}{i}") for i in range(nsem)]
            self.dcounts = [0] * nsem
        slot = self.dnext % len(self.dsems)
        self.dnext += 1
        sem = self.dsems[slot]
        key = f"d_{self.name}{slot}"
        if self.dcounts[slot] > 0:
            self._wait((key, sem, 16 * self.dcounts[slot]))
        for tok in self.ctx.barrier_toks:
            self._wait(tok)
        for tok in self._deps(reads, writes):
            self._wait(tok)
        self.eng.dma_start(out=out, in_=in_, **kw).then_inc(sem, 16)
        self.dcounts[slot] += 1
        tok = (key, sem, 16 * self.dcounts[slot])
        for b in writes:
            b.w = tok
            b.r = []
        for b in reads:
            if b not in writes:
                b.r.append(tok)
        self.ctx.all_dma.append((self, tok))
        return tok


class Ctx:
    def __init__(self, nc):
        self.nc = nc
        self.all_dma = []
        self.barrier_toks = []
        self.pe = Eng(self, nc.tensor, "pe", is_pe=True)
        self.act = Eng(self, nc.scalar, "act")
        self.dve = Eng(self, nc.vector, "dve")
        self.pool = Eng(self, nc.gpsimd, "pool")
        self.sp = Eng(self, nc.sync, "sp")
        self.out_toks = []

    def collective(self, kind, op, rg, in_ap, out_ap, wait=True):
        self.barrier()
        e = self.pool
        for tok in self.barrier_toks:
            e._wait(tok)
        if not hasattr(self, "cc_sem"):
            self.cc_sem = self.nc.alloc_semaphore(name="cc_sem")
            self.cc_count = 0
        e.eng.collective_compute(kind, op, replica_groups=rg, ins=[in_ap.opt()], outs=[out_ap.opt()]).then_inc(self.cc_sem)
        self.cc_count += 1
        self.cc_pending = ("cc", self.cc_sem, self.cc_count)
        if wait:
            self.collective_wait()

    def collective_wait(self):
        self.cc_tok = self.cc_pending
        self.barrier()

    def barrier(self):
        last = {}
        for eng, tok in self.all_dma:
            last[tok[0]] = tok
        for e in (self.pe, self.act, self.dve, self.pool):
            if e.count:
                last[e.name] = (e.name, e.sem, e.count)
        if getattr(self, "cc_tok", None) is not None:
            last["cc"] = self.cc_tok
        self.barrier_toks = list(last.values())

    def finish(self):
        last = {}
        for eng, tok in self.all_dma:
            last[tok[0]] = tok
        for tok in last.values():
            self.sp._wait(tok)
        for e in (self.pe, self.act, self.dve, self.pool):
            if e.count:
                self.sp._wait((e.name, e.sem, e.count))


S_LEN = 8192


class T:
    def __init__(self, ap, b=None):
        self.ap = ap
        self.b = b if b is not None else Buf()

    def __getitem__(self, k):
        return T(self.ap[k], self.b)

    def re(self, s, **kw):
        return T(self.ap.rearrange(s, **kw), self.b)


def tt(eng, out, a, b, op):
    return eng.op(lambda e: e.tensor_tensor(out.ap, a.ap, b.ap, op), reads=[a.b, b.b], writes=[out.b])


def ts(eng, out, a, s1, s2, op0, op1=None):
    rd = [a.b]
    s1a = s1
    s2a = s2
    if isinstance(s1, T):
        rd.append(s1.b)
        s1a = s1.ap
    if isinstance(s2, T):
        rd.append(s2.b)
        s2a = s2.ap
    if op1 is None:
        return eng.op(lambda e: e.tensor_scalar(out.ap, a.ap, s1a, s2a, op0), reads=rd, writes=[out.b])
    return eng.op(lambda e: e.tensor_scalar(out.ap, a.ap, s1a, s2a, op0, op1), reads=rd, writes=[out.b])


def stt(eng, out, a, s, b, op0, op1):
    rd = [a.b, b.b]
    sa = s
    if isinstance(s, T):
        rd.append(s.b)
        sa = s.ap
    return eng.op(lambda e: e.scalar_tensor_tensor(out.ap, a.ap, sa, b.ap, op0, op1), reads=rd, writes=[out.b])


def act(c, out, a, func, scale=None, bias=None):
    kw = {}
    if scale is not None:
        kw["scale"] = scale
    if bias is not None:
        kw["bias"] = bias
    return c.act.op(lambda e: e.activation(out.ap, a.ap, func, **kw), reads=[a.b], writes=[out.b])


def cp(eng, out, a):
    return eng.op(lambda e: e.tensor_copy(out.ap, a.ap), reads=[a.b], writes=[out.b])


def mm(c, out, lhsT, rhs, start, stop):
    return c.pe.op(lambda e: e.matmul(out.ap, lhsT.ap, rhs.ap, start=start, stop=stop),
                   reads=[lhsT.b, rhs.b], writes=[out.b])


def fourier_consts():
    s1 = np.arange(128)
    a = 2 * np.pi * np.outer(s1, s1) / 128
    fr = np.cos(a) / np.sqrt(128)
    fi = -np.sin(a) / np.sqrt(128)
    R1 = np.concatenate([fr, fi], 1).astype(np.float32)
    R2 = np.concatenate([-fi, fr], 1).astype(np.float32)
    s2 = np.arange(64)[:, None]
    k1 = np.arange(128)[None, :]
    th = 2 * np.pi * s2 * k1 / 8192
    cc = np.cos(th)
    ss = np.sin(th)
    CC = np.concatenate([cc, cc], 1).astype(np.float32)
    SS = np.concatenate([ss, -ss], 1).astype(np.float32)
    b = 2 * np.pi * np.outer(np.arange(64), np.arange(64)) / 64
    F3 = np.concatenate([np.cos(b) / 8, np.sin(b) / 8], 1).astype(np.float32)
    return {"fR1": R1, "fR2": R2, "fCC": CC, "fSS": SS, "fF3": F3}


def emit_fourier(nc, c, ps, Vh, consts, FfnT):
    with contextlib.ExitStack() as es:
        def S(name, shape, dt):
            return T(es.enter_context(nc.sbuf_tensor(name, shape, dt))[:])
        X = S("fX", [128, 64, 256], BF16)
        R1f = S("fR1f", [128, 256], F32)
        R2f = S("fR2f", [128, 256], F32)
        R1 = S("fR1b", [128, 256], BF16)
        R2 = S("fR2b", [128, 256], BF16)
        CC = S("fCCs", [64, 256], F32)
        SS = S("fSSs", [64, 256], F32)
        F3f = S("fF3f", [64, 128], F32)
        F3 = S("fF3b", [64, 128], BF16)
        A2 = S("fA2", [64, 128, 256], BF16)
        A1s = [S(f"fA1s{i}", [64, 2, 256], F32) for i in range(2)]
        t1 = [S(f"ft1{i}", [64, 2, 256], F32) for i in range(2)]
        t2 = [S(f"ft2{i}", [64, 2, 256], F32) for i in range(2)]
        OT = S("fOT", [128, S_LEN], BF16)

        c.sp.dma(X.ap, Vh.rearrange("(s1 s2) c -> s1 s2 c", s2=64), writes=[X.b])
        c.sp.dma(R1f.ap, consts["fR1"], writes=[R1f.b])
        c.sp.dma(R2f.ap, consts["fR2"], writes=[R2f.b])
        c.sp.dma(CC.ap, consts["fCC"], writes=[CC.b])
        c.sp.dma(SS.ap, consts["fSS"], writes=[SS.b])
        c.sp.dma(F3f.ap, consts["fF3"], writes=[F3f.b])
        cp(c.dve, R1, R1f)
        cp(c.dve, R2, R2f)
        cp(c.dve, F3, F3f)
        CCb = T(CC.ap.unsqueeze(1).to_broadcast([64, 2, 256]), CC.b)
        SSb = T(SS.ap.unsqueeze(1).to_broadcast([64, 2, 256]), SS.b)
        for fp in range(64):
            bank = ps.next()
            bv = T(bank.ap[0:64, :].rearrange("p (f c) -> p f c", f=2), bank.b)
            for j in range(2):
                f = fp * 2 + j
                mm(c, bv[:, j, :], X[:, :, f], R1, True, False)
                mm(c, bv[:, j, :], X[:, :, 128 + f], R2, False, True)
            i = fp % 2
            act(c, A1s[i], bv, AF.Copy)
            tt(c.dve, t1[i], A1s[i], CCb, ALU.mult)
            tt(c.pool, t2[i][:, :, 0:128], A1s[i][:, :, 128:256], SSb[:, :, 0:128], ALU.mult)
            tt(c.pool, t2[i][:, :, 128:256], A1s[i][:, :, 0:128], SSb[:, :, 128:256], ALU.mult)
            tt(c.dve, A2[:, fp * 2:fp * 2 + 2, :], t1[i], t2[i], ALU.add)
        OTv = OT.re("p (k2 k1) -> p k2 k1", k1=128)
        for kb in range(16):
            bank = ps.next()
            bv = T(bank.ap.rearrange("p (a k2) -> p a k2", a=8), bank.b)
            for a in range(8):
                k1 = kb * 8 + a
                mm(c, bv[:, a, :], A2[:, :, k1], F3[:, 0:64], True, False)
                mm(c, bv[:, a, :], A2[:, :, 128 + k1], F3[:, 64:128], False, True)
            src = T(bank.ap.rearrange("p (a k2) -> p k2 a", a=8), bank.b)
            if kb % 2 == 0:
                cp(c.dve, OTv[:, :, kb * 8:(kb + 1) * 8], src)
            else:
                act(c, OTv[:, :, kb * 8:(kb + 1) * 8], src, AF.Copy)
        c.sp.dma(FfnT, OT.ap, reads=[OT.b])


class PsumPool:
    def __init__(self, nc, es, n=8):
        self.banks = [T(es.enter_context(nc.psum_tensor(f"ps{i}", [128, 512], F32))[:]) for i in range(n)]
        self.i = 0

    def next(self):
        b = self.banks[self.i % len(self.banks)]
        self.i += 1
        return b


def build_fourier_test():
    nc = bass.Bass("TRN2", target_bir_lowering=False)
    Vh = nc.dram_tensor("Vh", [S_LEN, 256], BF16, kind="ExternalInput").ap()
    cn = {k: nc.dram_tensor(k, list(v.shape), F32, kind="ExternalInput").ap() for k, v in fourier_consts().items()}
    FfnT = nc.dram_tensor("FfnT", [128, S_LEN], BF16, kind="ExternalOutput").ap()
    c = Ctx(nc)
    with contextlib.ExitStack() as es:
        ps = PsumPool(nc, es)
        emit_fourier(nc, c, ps, Vh, cn, FfnT)
        c.finish()
    return nc


S_LEN = 8192
NEG = -30000.0


def na_mask_const():
    qc = np.arange(64)
    ws = np.clip(qc - 8, 0, 48)
    kc = np.arange(64)
    inwin = (kc[:, None] >= ws[None, :]) & (kc[:, None] < ws[None, :] + 16)
    m = np.where(inwin, 0.0, NEG).astype(np.float32)
    return np.concatenate([m, m], 0)


def na_bias_layout(rpb4):
    kc = np.arange(64)[:, None]
    qc = np.arange(64)[None, :]
    idx = np.clip(kc - qc + 15, 0, 30)
    return np.ascontiguousarray(rpb4[:, :, idx])


def emit_na(nc, c, psS, psV, qTh, kTh, vh, biasT, maskc, OnaT, nrows=128):
    with contextlib.ExitStack() as es:
        def S(name, shape, dt):
            return T(es.enter_context(nc.sbuf_tensor(name, shape, dt))[:])
        MM = S("nMM", [128, 4, 14, 64], F32)
        msk = S("nmsk", [128, 64], F32)
        ones = S("nones", [128, 64], BF16)
        qT = S("nqT", [128, S_LEN], BF16)
        kT = S("nkT", [128, S_LEN], BF16)
        vE = S("nvE", [128, 64, 128], BF16)
        vO = S("nvO", [128, 64, 128], BF16)
        Ot = [S(f"nO{i}", [64, S_LEN], BF16) for i in range(2)]
        sb = [S(f"nsb{i}", [128, 4, 64], F32) for i in range(3)]
        ex = [S(f"nex{i}", [128, 4, 64], BF16) for i in range(3)]
        rB = [S(f"nrB{i}", [64, 4, 64], F32) for i in range(2)]

        c.pool.op(lambda e: e.memset(ones.ap, 1.0), writes=[ones.b])
        c.sp.dma(msk.ap, maskc, writes=[msk.b])
        for h in range(4):
            c.sp.dma(MM.ap[0:64, h, :, :], biasT[h, 0:14].rearrange("dr kc qc -> kc dr qc"), writes=[MM.b])
            c.sp.dma(MM.ap[64:128, h, :, :], biasT[h, 1:15].rearrange("dr kc qc -> kc dr qc"), writes=[MM.b])
        MMf = MM.re("p h d q -> p (h d) q")
        tt(c.dve, MMf, MMf, T(msk.ap.unsqueeze(1).to_broadcast([128, 56, 64]), msk.b), ALU.add)

        it = 0
        for p in range(2):
            c.sp.dma(qT.ap, qTh[p], writes=[qT.b])
            c.sp.dma(kT.ap, kTh[p], writes=[kT.b])
            c.sp.dma(vE.ap, vh[:, p * 128:(p + 1) * 128].rearrange("(j p) c -> p j c", p=128), writes=[vE.b])
            c.sp.dma(vO.ap[:, 0:63, :], vh[64:64 + 63 * 128, p * 128:(p + 1) * 128].rearrange("(j p) c -> p j c", p=128),
                     writes=[vO.b])
            for hp in range(2):
                hh = p * 2 + hp
                O = Ot[hp]
                psl = slice(hp * 64, (hp + 1) * 64)
                pv = None
                for r in range(nrows):
                    start = min(max(r - 4, 0), 120)
                    dr0 = start - r + 7
                    sbank = psS.next()
                    sv = T(sbank.ap[:, 0:256].rearrange("p (k q) -> p k q", k=4), sbank.b)
                    for kt in range(4):
                        k0 = (start + 2 * kt) * 64
                        mm(c, sv[:, kt, :], kT[psl, k0:k0 + 128], qT[psl, r * 64:(r + 1) * 64], True, True)
                    i = it % 3
                    it += 1
                    tt(c.dve, sb[i], sv, MM[:, hh, dr0:dr0 + 7:2, :], ALU.add)
                    act(c, ex[i], sb[i], AF.Exp)
                    rr = r % 4
                    if rr == 0:
                        pvb = psV.next()
                        pv = T(pvb.ap[0:64, :].rearrange("p (r a q) -> p r a q", r=4, a=2), pvb.b)
                    vX = vE if start % 2 == 0 else vO
                    for kt in range(4):
                        j = (start + 2 * kt) // 2
                        mm(c, pv[:, rr, 0, :], vX[:, j, psl], ex[i][:, kt, :], kt == 0, kt == 3)
                    for kt in range(4):
                        mm(c, pv[:, rr, 1, :], ones, ex[i][:, kt, :], kt == 0, kt == 3)
                    if rr == 3:
                        rb = rB[(r // 4) % 2]
                        c.dve.op(lambda e, rb=rb, pv=pv: e.reciprocal(rb.ap, pv.ap[:, :, 1, :]), reads=[pv.b], writes=[rb.b])
                        tt(c.dve, T(O.ap[:, (r - 3) * 64:(r + 1) * 64].rearrange("p (r q) -> p r q", r=4), O.b),
                           pv[:, :, 0, :], rb, ALU.mult)
                c.sp.dma(OnaT[hh * 64:(hh + 1) * 64, 0:nrows * 64], O.ap[:, 0:nrows * 64], reads=[O.b])


def build_na_test(nrows=128):
    nc = bass.Bass("TRN2", target_bir_lowering=False)
    qTh = nc.dram_tensor("qTh", [2, 128, S_LEN], BF16, kind="ExternalInput").ap()
    kTh = nc.dram_tensor("kTh", [2, 128, S_LEN], BF16, kind="ExternalInput").ap()
    vh = nc.dram_tensor("vh", [S_LEN, 256], BF16, kind="ExternalInput").ap()
    biasT = nc.dram_tensor("biasT", [4, 15, 64, 64], F32, kind="ExternalInput").ap()
    maskc = nc.dram_tensor("maskc", [128, 64], F32, kind="ExternalInput").ap()
    OnaT = nc.dram_tensor("OnaT", [256, S_LEN], BF16, kind="ExternalOutput").ap()
    c = Ctx(nc)
    with contextlib.ExitStack() as es:
        ps = PsumPool(nc, es)
        psS = PsumPool.__new__(PsumPool)
        psS.banks = ps.banks[0:6]
        psS.i = 0
        psV = PsumPool.__new__(PsumPool)
        psV.banks = ps.banks[6:8]
        psV.i = 0
        emit_na(nc, c, psS, psV, qTh, kTh, vh, biasT, maskc, OnaT, nrows)
        c.finish()
    return nc


I32 = mybir.dt.int32
NCH = 1024
TWO_PI = float(2 * np.pi)
PI_LO = 3.1415925


def s5_consts():
    t = np.repeat(np.arange(8), 16).astype(np.float32)
    EA = np.zeros((128, 128), np.float32)
    EA[:, 0:64] = (7 - t)[:, None]
    EA[:, 64:128] = t[:, None]
    EBP = np.zeros((128, 128), np.float32)
    EBP[0:64, :] = -t[None, :]
    EBP[64:128, :] = t[None, :]
    EBQ = -EBP
    EBO = np.zeros((128, 128), np.float32)
    EBO[0:64, :] = (t + 1)[None, :]
    EBO[64:128, :] = (8 - t)[None, :]
    MF = (t[:, None] <= t[None, :]).astype(np.float32)
    MB = (t[:, None] >= t[None, :]).astype(np.float32)
    return np.stack([EA, EBP, EBQ, EBO, MF, MB])


def s5_param_layout(a_re, a_im, log_dt, b_re, b_im, c_re, c_im, d_skip, gsel):
    G = list(gsel)
    pA = np.zeros((5, 128, 8, 128), np.float32)
    pB = np.zeros((4, 128, 8, 128), np.float32)
    pS = np.zeros((3, 128, 8), np.float32)
    dsk = np.zeros((128, 8), np.float32)
    for gi, g in enumerate(G):
        for d in range(2):
            sl = slice(d * 64, (d + 1) * 64)
            pA[0, :, gi, sl] = a_re[d, g][None, :]
            pA[1, :, gi, sl] = a_im[d, g][None, :]
            pA[2, :, gi, sl] = log_dt[d, g]
            pA[3, :, gi, sl] = np.tile(b_re[d, g].T, (8, 1))
            pA[4, :, gi, sl] = np.tile(b_im[d, g].T, (8, 1))
            pB[0, sl, gi, :] = np.tile(b_re[d, g], (1, 8))
            pB[1, sl, gi, :] = np.tile(b_im[d, g], (1, 8))
            pB[2, sl, gi, :] = np.tile(c_re[d, g].T, (1, 8))
            pB[3, sl, gi, :] = np.tile(c_im[d, g].T, (1, 8))
            pS[0, sl, gi] = a_re[d, g]
            pS[1, sl, gi] = a_im[d, g]
            pS[2, sl, gi] = log_dt[d, g]
        dsk[:, gi] = np.tile(d_skip[g * 16:(g + 1) * 16], 8)
    return pA, pB, pS, dsk


class Scratch:
    def __init__(self, nc, es, prefix, n, shape, dt=F32):
        self.free = [T(es.enter_context(nc.sbuf_tensor(f"{prefix}{i}", shape, dt))[:]) for i in range(n)]

    def get(self):
        return self.free.pop()

    def put(self, *ts_):
        for t in ts_:
            self.free.append(t)


def sincos(c, ang, sn, cs, sc, sci):
    r = sc.get()
    for (out, shift) in ((sn, 0.0), (cs, float(np.pi / 2))):
        if shift == 0.0:
            ts(c.dve, sci, ang, 1.0 / TWO_PI, None, ALU.mult)
        else:
            ts(c.dve, sci, ang, shift, 1.0 / TWO_PI, ALU.add, ALU.mult)
        cp(c.dve, r, sci)
        if shift == 0.0:
            stt(c.dve, r, r, -TWO_PI, ang, ALU.mult, ALU.add)
        else:
            stt(c.dve, r, r, -TWO_PI, ang, ALU.mult, ALU.add)
            ts(c.dve, r, r, shift, None, ALU.add)
        ts(c.dve, r, r, -PI_LO, PI_LO, ALU.max, ALU.min)
        act(c, out, r, AF.Sin)
    sc.put(r)


def cpow(c, E, ar, ai, pr, pi, sc, sci):
    mg = sc.get()
    ang = sc.get()
    sn = sc.get()
    if E is None:
        act(c, mg, ar, AF.Exp)
        cp(c.dve, ang, ai)
    elif isinstance(E, float):
        act(c, mg, ar, AF.Exp, scale=E)
        ts(c.dve, ang, ai, E, None, ALU.mult)
    else:
        tt(c.dve, mg, E, ar, ALU.mult)
        act(c, mg, mg, AF.Exp)
        tt(c.dve, ang, E, ai, ALU.mult)
    sincos(c, ang, sn, pr, sc, sci)
    tt(c.dve, pi, mg, sn, ALU.mult)
    tt(c.dve, pr, mg, pr, ALU.mult)
    sc.put(mg, ang, sn)


def cmul(c, eng, or_, oi, ar, ai, br, bi, sc, neg_imag=False):
    t1 = sc.get()
    t2 = sc.get()
    tt(eng, t1, ar, br, ALU.mult)
    tt(eng, t2, ai, bi, ALU.mult)
    tt(eng, or_, t1, t2, ALU.subtract)
    tt(eng, t1, ar, bi, ALU.mult)
    tt(eng, t2, ai, br, ALU.mult)
    if neg_imag:
        stt(c.dve, oi, t1, -1.0, t2, ALU.mult, ALU.subtract)
    else:
        tt(eng, oi, t1, t2, ALU.add)
    sc.put(t1, t2)


def coef(c, lbr, lbi, are, aim, cr, ci, sc):
    den = sc.get()
    t1 = sc.get()
    ts(c.dve, lbr, lbr, -1.0, None, ALU.add)
    tt(c.dve, den, are, are, ALU.mult)
    tt(c.dve, t1, aim, aim, ALU.mult)
    tt(c.dve, den, den, t1, ALU.add)
    c.dve.op(lambda e: e.reciprocal(den.ap, den.ap), reads=[den.b], writes=[den.b])
    tt(c.dve, cr, lbr, are, ALU.mult)
    tt(c.dve, t1, lbi, aim, ALU.mult)
    tt(c.dve, cr, cr, t1, ALU.add)
    tt(c.dve, cr, cr, den, ALU.mult)
    tt(c.dve, ci, lbi, are, ALU.mult)
    tt(c.dve, t1, lbr, aim, ALU.mult)
    tt(c.dve, ci, ci, t1, ALU.subtract)
    tt(c.dve, ci, ci, den, ALU.mult)
    sc.put(den, t1)


def s5_cidx():
    ci = np.zeros((128, NCH), np.float32)
    ci[0:64] = np.arange(NCH, dtype=np.float32)[None, :]
    ci[64:128] = np.arange(NCH - 1, -1, -1, dtype=np.float32)[None, :]
    return ci


def phase_reduce(c, out, ang, tmpf, tmpi):
    ts(c.dve, tmpi, ang, 1.0 / TWO_PI, None, ALU.mult)
    cp(c.dve, tmpf, tmpi)
    stt(c.dve, out, tmpf, -TWO_PI, ang, ALU.mult, ALU.add)


def scan_hw(nc, c, es, XSt, XSf, XSb, RHO, PHI, cId, nch):
    def S(name, shape, dt):
        return T(es.enter_context(nc.sbuf_tensor(name, shape, dt))[:])
    CI = S("hCI", [128, nch], F32)
    c.sp.dma(CI.ap, cId, writes=[CI.b])
    NB = 2
    ang = [S(f"hang{i}", [128, nch], F32) for i in range(NB)]
    red = [S(f"hred{i}", [128, nch], F32) for i in range(NB)]
    ki = [S(f"hki{i}", [128, nch], I32) for i in range(NB)]
    cs = [S(f"hcs{i}", [128, nch], F32) for i in range(NB)]
    sn = [S(f"hsn{i}", [128, nch], F32) for i in range(NB)]

    def halves(name):
        t = es.enter_context(nc.sbuf_tensor(name, [128, nch], F32))[:]
        return [T(t[0:64], Buf()), T(t[64:128], Buf())]
    t1 = [halves(f"ht1_{i}") for i in range(NB)]
    t2 = [halves(f"ht2_{i}") for i in range(NB)]
    zr = [halves(f"hzr_{i}") for i in range(NB)]
    zi = [halves(f"hzi_{i}") for i in range(NB)]
    yr = [halves(f"hyr_{i}") for i in range(NB)]
    yi = [halves(f"hyi_{i}") for i in range(NB)]
    HALF_PI = float(np.pi / 2)
    for g in range(8):
        q = g % NB
        ts(c.dve, ang[q], CI, PHI[:, g:g + 1], None, ALU.mult)
        phase_reduce(c, red[q], ang[q], red[q], ki[q])
        ts(c.dve, red[q], red[q], -PI_LO, PI_LO, ALU.max, ALU.min)
        act(c, sn[q], red[q], AF.Sin)
        stt(c.dve, ang[q], red[q], -1.0, red[q], ALU.mult, ALU.max)
        act(c, cs[q], ang[q], AF.Sin, scale=-1.0, bias=HALF_PI)
        for h, eng, XS in ((0, c.dve, XSf), (1, c.pool, XSb)):
            rows = slice(h * 64, (h + 1) * 64)
            c0 = 2 if h == 0 else 0
            Sr = T(XSt[rows, g, 0, c0:c0 + nch], XS.b)
            Si = T(XSt[rows, g, 1, c0:c0 + nch], XS.b)
            csh = cs[q][rows]
            snh = sn[q][rows]
            a, b = t1[q][h], t2[q][h]
            tt(eng, a, Sr, csh, ALU.mult)
            tt(eng, b, Si, snh, ALU.mult)
            tt(eng, zr[q][h], a, b, ALU.add)
            tt(eng, a, Si, csh, ALU.mult)
            tt(eng, b, Sr, snh, ALU.mult)
            tt(eng, zi[q][h], a, b, ALU.subtract)
        for h in range(2):
            rows = slice(h * 64, (h + 1) * 64)
            rho_bc = RHO.ap[rows, g:g + 1].to_broadcast([64, nch])
            for z, y in ((zr[q][h], yr[q][h]), (zi[q][h], yi[q][h])):
                if h == 0:
                    c.dve.op(lambda e, z=z, y=y, rho_bc=rho_bc: e.tensor_tensor_scan(y.ap, rho_bc, z.ap, 0.0, ALU.mult, ALU.add),
                             reads=[z.b, RHO.b], writes=[y.b])
                else:
                    c.dve.op(lambda e, z=z, y=y, rho_bc=rho_bc: e.tensor_tensor_scan(y.ap[:, ::-1], rho_bc, z.ap[:, ::-1], 0.0,
                                                                                       ALU.mult, ALU.add),
                             reads=[z.b, RHO.b], writes=[y.b])
        for h, eng, XS in ((0, c.pool, XSf), (1, c.pool, XSb)):
            rows = slice(h * 64, (h + 1) * 64)
            c0 = 2 if h == 0 else 0
            Xr = T(XSt[rows, g, 0, c0:c0 + nch], XS.b)
            Xi = T(XSt[rows, g, 1, c0:c0 + nch], XS.b)
            csh = cs[q][rows]
            snh = sn[q][rows]
            a, b = t1[q][h], t2[q][h]
            tt(eng, a, yr[q][h], csh, ALU.mult)
            tt(eng, b, yi[q][h], snh, ALU.mult)
            tt(eng, Xr, a, b, ALU.subtract)
            tt(eng, a, yi[q][h], csh, ALU.mult)
            tt(eng, b, yr[q][h], snh, ALU.mult)
            tt(eng, Xi, a, b, ALU.add)


STOP = [99]
KD = [0, 1]


def emit_s5(nc, c, ps, Uh, pA, pB, pS, cE, dskd, Ypre, nch=NCH, cId=None):
    FS = [128, 8, 128]
    with contextlib.ExitStack() as es:
        def S(name, shape, dt):
            return T(es.enter_context(nc.sbuf_tensor(name, shape, dt))[:])
        WsR = S("sWsR", FS, BF16)
        WsI = S("sWsI", FS, BF16)
        Kg = S("sKg", FS, BF16)
        Or = S("sOr", FS, F32)
        nOi = S("snOi", FS, F32)
        MR = S("sMR", [128, 8, 2], F32)
        MI = S("sMI", [128, 8], F32)
        nMI = S("snMI", [128, 8], F32)
        RHO = S("sRHO", [128, 8], F32)
        PHI = S("sPHI", [128, 8], F32)
        dsk = S("sdsk", [128, 8], F32)
        U = S("sU", [128, 8, nch], BF16)
        XSt = es.enter_context(nc.sbuf_tensor("sXS", [128, 8, 2, nch + 2], F32))[:]
        XSf = T(XSt[0:64], Buf("XSf"))
        XSb = T(XSt[64:128], Buf("XSb"))
        XSbufs = [XSf.b, XSb.b]
        Yst = [S(f"sY{i}", [128, 512], BF16) for i in range(2)]
        cEt = S("scE", [128, 6, 128], F32)

        c.sp.dma(U.ap, Uh.rearrange("g p n -> p g n"), writes=[U.b])
        c.sp.dma(dsk.ap, dskd, writes=[dsk.b])
        c.sp.dma(cEt.ap, cE.rearrange("e p n -> p e n"), writes=[cEt.b])

        def Eb(i):
            return T(cEt.ap[:, i, :].unsqueeze(1).to_broadcast(FS), cEt.b)

        with contextlib.ExitStack() as esA:
            sc = Scratch(nc, esA, "sA", 14, FS)
            sci = T(esA.enter_context(nc.sbuf_tensor("sAi", FS, I32))[:])
            pAt = T(esA.enter_context(nc.sbuf_tensor("spA", [128, 5, 8, 128], F32))[:])
            c.sp.dma(pAt.ap, pA.rearrange("e p g n -> p e g n"), writes=[pAt.b])
            are, aim, ldt, bre, bim = [pAt[:, i] for i in range(5)]
            dt = sc.get()
            ar = sc.get()
            ai = sc.get()
            act(c, dt, ldt, AF.Exp)
            tt(c.dve, ar, are, dt, ALU.mult)
            tt(c.dve, ai, aim, dt, ALU.mult)
            lbr = sc.get()
            lbi = sc.get()
            cpow(c, None, ar, ai, lbr, lbi, sc, sci)
            cr = sc.get()
            ci = sc.get()
            coef(c, lbr, lbi, are, aim, cr, ci, sc)
            Bbr = lbr
            Bbi = lbi
            cmul(c, c.dve, Bbr, Bbi, cr, ci, bre, bim, sc)
            pr = cr
            pi = ci
            cpow(c, Eb(0), ar, ai, pr, pi, sc, sci)
            cmul(c, c.dve, WsR, WsI, pr, pi, Bbr, Bbi, sc)
        c.barrier()

        if STOP[0] < 1:
            return
        with contextlib.ExitStack() as esB:
            sc = Scratch(nc, esB, "sB", 12, FS)
            scs = Scratch(nc, esB, "sBs", 14, [128, 8])
            sci = T(esB.enter_context(nc.sbuf_tensor("sBi", FS, I32))[:])
            scis = T(esB.enter_context(nc.sbuf_tensor("sBis", [128, 8], I32))[:])
            pBt = T(esB.enter_context(nc.sbuf_tensor("spB", [128, 4, 8, 128], F32))[:])
            pSt = T(esB.enter_context(nc.sbuf_tensor("spS", [128, 3, 8], F32))[:])
            c.sp.dma(pBt.ap, pB.rearrange("e p g n -> p e g n"), writes=[pBt.b])
            c.sp.dma(pSt.ap, pS.rearrange("e p g -> p e g"), writes=[pSt.b])
            bre, bim, cre, cim = [pBt[:, i] for i in range(4)]
            sare, saim, sldt = [pSt[:, i] for i in range(3)]
            dts = scs.get()
            ars = scs.get()
            ais = scs.get()
            act(c, dts, sldt, AF.Exp)
            tt(c.dve, ars, sare, dts, ALU.mult)
            tt(c.dve, ais, saim, dts, ALU.mult)
            lbr = scs.get()
            lbi = scs.get()
            cpow(c, None, ars, ais, lbr, lbi, scs, scis)
            crs = scs.get()
            cis = scs.get()
            coef(c, lbr, lbi, sare, saim, crs, cis, scs)
            mur = scs.get()
            mui = scs.get()
            cpow(c, 8.0, ars, ais, mur, mui, scs, scis)
            cp(c.dve, MR[:, :, 0], mur)
            cp(c.dve, MR[:, :, 1], mur)
            cp(c.dve, MI, mui)
            ts(c.dve, nMI, mui, -1.0, None, ALU.mult)
            act(c, RHO, ars, AF.Exp, scale=8.0)
            ts(c.dve, mur, ais, 8.0, None, ALU.mult)
            phase_reduce(c, PHI, mur, mui, scis)

            if STOP[0] < 1.1:
                return

            def bc(t):
                return T(t.ap.unsqueeze(2).to_broadcast(FS), t.b)
            Bbr = sc.get()
            Bbi = sc.get()
            cmul(c, c.dve, Bbr, Bbi, bc(crs), bc(cis), bre, bim, sc)
            pr = sc.get()
            pi = sc.get()
            PTr = sc.get()
            PTi = sc.get()
            cpow(c, Eb(1), bc(ars), bc(ais), pr, pi, sc, sci)
            cmul(c, c.dve, PTr, PTi, pr, pi, Bbr, Bbi, sc)
            if STOP[0] < 1.2:
                return
            sc.put(Bbr, Bbi)
            QQr = sc.get()
            nQQi = sc.get()
            cpow(c, Eb(2), bc(ars), bc(ais), pr, pi, sc, sci)
            cmul(c, c.dve, QQr, nQQi, pr, pi, cre, cim, sc, neg_imag=True)
            cpow(c, Eb(3), bc(ars), bc(ais), pr, pi, sc, sci)
            cmul(c, c.dve, Or, nOi, pr, pi, cre, cim, sc, neg_imag=True)
            if STOP[0] < 1.3:
                return
            tmp1 = pr
            tmp2 = pi
            MF = T(cEt.ap[:, 4, :], cEt.b)
            MBk = T(cEt.ap[:, 5, :], cEt.b)
            rmask = T(cEt.ap[:, 4, 127:128], cEt.b)
            Qd = [[sc.get(), sc.get()], [sc.get(), sc.get()]]
            for d in range(2):
                osl = slice((1 - d) * 64, (2 - d) * 64)
                ksl = slice(d * 64, (d + 1) * 64)
                for q, src in ((0, QQr), (1, nQQi)):
                    c.dve.op(lambda e, t=Qd[d][q], osl=osl: e.memset(t.ap[osl], 0.0), writes=[Qd[d][q].b])
                    cp(c.dve, Qd[d][q][ksl], src[ksl])
            for g in range(8):
                bank = ps.next()
                for d in KD:
                    o = bank[:, d * 128:(d + 1) * 128]
                    mm(c, o, PTr[:, g, :], Qd[d][0][:, g, :], True, False)
                    mm(c, o, PTi[:, g, :], Qd[d][1][:, g, :], False, True)
                tt(c.dve, tmp1[:, g, :], bank[:, 0:128], MF, ALU.mult)
                tt(c.dve, tmp2[:, g, :], bank[:, 128:256], MBk, ALU.mult)
                tt(c.dve, Kg[:, g, :], tmp1[:, g, :], tmp2[:, g, :], ALU.add)
        c.barrier()

        if STOP[0] < 2:
            return
        XSall = T(XSt, None)
        c.dve.op(lambda e: e.memset(XSt[0:64, :, :, 1:2], 0.0), writes=[XSf.b])
        c.dve.op(lambda e: e.memset(XSt[64:128, :, :, nch:nch + 1], 0.0), writes=[XSb.b])
        ncb = nch // 512
        for g in range(8):
            for cb in range(ncb):
                csl = slice(cb * 512, (cb + 1) * 512)
                xsl = slice(1 + cb * 512, 1 + (cb + 1) * 512)
                bR = ps.next()
                mm(c, bR, WsR[:, g, :], U[:, g, csl], True, True)
                bI = ps.next()
                mm(c, bI, WsI[:, g, :], U[:, g, csl], True, True)
                fsl = slice(2 + cb * 512, 2 + (cb + 1) * 512)
                bsl = slice(cb * 512, (cb + 1) * 512)
                c.act.op(lambda e, bR=bR, g=g, fsl=fsl: e.activation(XSt[0:64, g, 0, fsl], bR.ap[0:64], AF.Copy),
                         reads=[bR.b], writes=[XSf.b])
                c.act.op(lambda e, bR=bR, g=g, bsl=bsl: e.activation(XSt[64:128, g, 0, bsl], bR.ap[64:128], AF.Copy),
                         reads=[bR.b], writes=[XSb.b])
                c.dve.op(lambda e, bI=bI, g=g, fsl=fsl: e.tensor_copy(XSt[0:64, g, 1, fsl], bI.ap[0:64]),
                         reads=[bI.b], writes=[XSf.b])
                c.dve.op(lambda e, bI=bI, g=g, bsl=bsl: e.tensor_copy(XSt[64:128, g, 1, bsl], bI.ap[64:128]),
                         reads=[bI.b], writes=[XSb.b])
        if STOP[0] < 3:
            return
        if cId is not None:
            scan_hw(nc, c, es, XSt, XSf, XSb, RHO, PHI, cId, nch)
        else:
            tA = [S("stAf", [128, 8, 2], F32), S("stAb", [128, 8, 2], F32)]
            tB = [S("stBf", [128, 8, 2], F32), S("stBb", [128, 8, 2], F32)]
            for step in range(nch):
                for d, eng, XS in ((0, c.dve, XSf), (1, c.pool, XSb)):
                    psl = slice(d * 64, (d + 1) * 64)
                    if d == 0:
                        j, jp = step + 2, step + 1
                    else:
                        j, jp = nch - 1 - step, nch - step
                    xp = T(XSt[psl, :, :, jp], XS.b)
                    xj = T(XSt[psl, :, :, j], XS.b)
                    a = tA[d][psl]
                    b = tB[d][psl]
                    tt(eng, a, xp, MR[psl], ALU.mult)
                    tt(eng, b[:, :, 0], xp[:, :, 1], nMI[psl], ALU.mult)
                    tt(eng, b[:, :, 1], xp[:, :, 0], MI[psl], ALU.mult)
                    tt(eng, a, a, b, ALU.add)
                    tt(eng, xj, xj, a, ALU.add)
        if STOP[0] < 4:
            return
        it = 0
        for g in range(8):
            for cb in range(ncb):
                csl = slice(cb * 512, (cb + 1) * 512)
                yb = ps.next()
                mm(c, yb, Kg[:, g, :], U[:, g, csl], True, False)
                f0 = cb * 512
                xr = T(XSt[:, g, 0, f0 + 1:f0 + 513], XSf.b)
                xi = T(XSt[:, g, 1, f0 + 1:f0 + 513], XSb.b)
                c.pe.op(lambda e, yb=yb, g=g, xr=xr: e.matmul(yb.ap, Or.ap[:, g, :], xr.ap, start=False, stop=False),
                        reads=[Or.b, XSf.b, XSb.b], writes=[yb.b])
                c.pe.op(lambda e, yb=yb, g=g, xi=xi: e.matmul(yb.ap, nOi.ap[:, g, :], xi.ap, start=False, stop=True),
                        reads=[nOi.b, XSf.b, XSb.b], writes=[yb.b])
                y = Yst[it % 2]
                it += 1
                stt(c.dve, y, U[:, g, csl], dsk[:, g:g + 1], yb, ALU.mult, ALU.add)
                c.sp.dma(Ypre[g, :, csl], y.ap, reads=[y.b])


def build_s5_test(nch=NCH):
    nc = bass.Bass("TRN2", target_bir_lowering=False)
    Uh = nc.dram_tensor("Uh", [8, 128, nch], BF16, kind="ExternalInput").ap()
    pA = nc.dram_tensor("pA", [5, 128, 8, 128], F32, kind="ExternalInput").ap()
    pB = nc.dram_tensor("pB", [4, 128, 8, 128], F32, kind="ExternalInput").ap()
    pS = nc.dram_tensor("pS", [3, 128, 8], F32, kind="ExternalInput").ap()
    cE = nc.dram_tensor("cE", [6, 128, 128], F32, kind="ExternalInput").ap()
    dskd = nc.dram_tensor("dsk", [128, 8], F32, kind="ExternalInput").ap()
    cId = nc.dram_tensor("cI", [128, nch], F32, kind="ExternalInput").ap()
    Ypre = nc.dram_tensor("Ypre", [8, 128, nch], BF16, kind="ExternalOutput").ap()
    c = Ctx(nc)
    with contextlib.ExitStack() as es:
        ps = PsumPool(nc, es)
        emit_s5(nc, c, ps, Uh, pA, pB, pS, cE, dskd, Ypre, nch, cId)
        c.finish()
    return nc


NT = 4096
TT = 512
NTT = NT // TT
D = 1024
DIN = 5120
EPS = 1e-6


def build_l1():
    nc = bass.Bass("TRN2", target_bir_lowering=False)
    xT = nc.dram_tensor("xT", [8, 128, NT], F32, kind="ExternalInput").ap()
    w_in = nc.dram_tensor("w_in", [D, DIN], F32, kind="ExternalInput").ap()
    g_mix = nc.dram_tensor("g_mix", [128, 8], F32, kind="ExternalInput").ap()
    wc = nc.dram_tensor("wc", [256, 512], F32, kind="ExternalInput").ap()
    Vo = nc.dram_tensor("V", [NT, 512], BF16, kind="ExternalOutput").ap()
    qTo = nc.dram_tensor("qT", [512, NT], BF16, kind="ExternalOutput").ap()
    kTo = nc.dram_tensor("kT", [512, NT], BF16, kind="ExternalOutput").ap()
    vo = nc.dram_tensor("v", [NT, 512], BF16, kind="ExternalOutput").ap()
    Uo = nc.dram_tensor("U", [16, 128, NT // 8], BF16, kind="ExternalOutput").ap()
    gTo = nc.dram_tensor("gT", [3072, NT], BF16, kind="ExternalOutput").ap()
    emit_l1(nc, xT, w_in, g_mix, wc, Vo, qTo, kTo, vo, Uo, gTo)
    return nc


def emit_l1(nc, xT, w_in, g_mix, wc, Vo, qTo, kTo, vo, Uo, gTo):
    c = Ctx(nc)
    sb = lambda name, shape, dt: nc.alloc_sbuf_tensor(name, shape, dt) if hasattr(nc, "alloc_sbuf_tensor") else None
    import contextlib
    es = contextlib.ExitStack()
    with es:
        def S(name, shape, dt):
            return es.enter_context(nc.sbuf_tensor(name, shape, dt))

        def P(name, shape, dt):
            return es.enter_context(nc.psum_tensor(name, shape, dt))

        wbf = S("wbf", [128, 8, DIN], BF16)
        wbf_b = Buf("wbf")
        wst = [S(f"wst{i}", [128, 8, 640], F32) for i in range(2)]
        wst_b = [Buf(f"wst{i}") for i in range(2)]
        gm = S("gm", [128, 8], F32)
        gm_b = Buf("gm")
        wcf = S("wcf", [128, 2, 512], F32)
        wcb = S("wcb", [128, 2, 512], BF16)
        wcf_b, wcb_b = Buf(), Buf()
        ones = S("ones", [128, 128], BF16)
        ones_b = Buf()
        xt = [S(f"xt{i}", [128, 8, TT], F32) for i in range(2)]
        xt_b = [Buf() for i in range(2)]
        xsq = S("xsq", [128, 8, TT], BF16)
        xsq_b = Buf()
        hT = [S(f"hT{i}", [128, 8, TT], BF16) for i in range(2)]
        hT_b = [Buf() for i in range(2)]
        rstd = S("rstd", [128, TT], F32)
        rstd_b = Buf()
        ufn = S("ufn", [128, 2, TT], BF16)
        ufn_b = Buf()
        NST = 6
        st = [S(f"st{i}", [128, TT], BF16) for i in range(NST)]
        st_b = [Buf() for i in range(NST)]
        ps = [P(f"ps{i}", [128, 512], F32) for i in range(8)]
        ps_b = [Buf() for i in range(8)]
        psi = [0]
        sti = [0]

        def next_ps():
            i = psi[0] % 8
            psi[0] += 1
            return ps[i], ps_b[i]

        def next_st():
            i = sti[0] % NST
            sti[0] += 1
            return st[i], st_b[i]

        c.sp.dma(gm[:], g_mix, writes=[gm_b])
        c.sp.dma(wcf[:], wc.rearrange("(j p) c -> p j c", p=128), writes=[wcf_b])
        c.dve.op(lambda e: e.tensor_copy(wcb[:], wcf[:]), reads=[wcf_b], writes=[wcb_b])
        c.pool.op(lambda e: e.memset(ones[:], 1.0), writes=[ones_b])
        w_v = w_in.rearrange("(k p) c -> p k c", p=128)
        for pc in range(8):
            i = pc % 2
            c.sp.dma(wst[i][:], w_v[:, :, pc * 640:(pc + 1) * 640], writes=[wst_b[i]])
            for k in range(8):
                eng = c.dve if k % 2 == 0 else c.pool
                eng.op(lambda e, i=i, k=k, pc=pc: e.tensor_scalar(
                    wbf[:, k, pc * 640:(pc + 1) * 640], wst[i][:, k, :], gm[:, k:k + 1], None, ALU.mult),
                    reads=[wst_b[i], gm_b], writes=[wbf_b])

        dmaq = [c.sp, c.act]
        dqi = [0]

        def store(out_ap, in_ap, b):
            q = c.sp
            q.dma(out_ap, in_ap, reads=[b])

        for tt in range(NTT):
            xi = tt % 2
            tsl = slice(tt * TT, (tt + 1) * TT)
            c.sp.dma(xt[xi][:], xT.rearrange("k p n -> p k n")[:, :, tsl], writes=[xt_b[xi]])
            c.act.op(lambda e, xi=xi: e.activation(xsq[:], xt[xi][:], AF.Square), reads=[xt_b[xi]], writes=[xsq_b])
            pt, pb = next_ps()
            for k in range(8):
                c.pe.op(lambda e, k=k, pt=pt: e.matmul(pt[:], ones[:], xsq[:, k, :], start=(k == 0), stop=(k == 7)),
                        reads=[ones_b, xsq_b], writes=[pb])
            c.act.op(lambda e, pt=pt: e.activation(rstd[:], pt[:], AF.Sqrt, scale=1.0 / D, bias=EPS),
                     reads=[pb], writes=[rstd_b])
            c.dve.op(lambda e: e.reciprocal(rstd[:], rstd[:]),
                     reads=[rstd_b], writes=[rstd_b])
            for k in range(8):
                eng = c.dve if k % 2 == 0 else c.pool
                eng.op(lambda e, k=k, xi=xi: e.tensor_tensor(hT[xi][:, k, :], xt[xi][:, k, :], rstd[:], ALU.mult),
                       reads=[xt_b[xi], rstd_b], writes=[hT_b[xi]])
            h = hT[xi]
            hb = hT_b[xi]
            for fb in range(40):
                if 10 <= fb < 14:
                    continue
                pt, pb = next_ps()
                for k in range(8):
                    c.pe.op(lambda e, k=k, pt=pt, fb=fb: e.matmul(pt[:], wbf[:, k, fb * 128:(fb + 1) * 128], h[:, k, :],
                                                                   start=(k == 0), stop=(k == 7)),
                            reads=[wbf_b, hb], writes=[pb])
                if fb < 2:
                    c.dve.op(lambda e, pt=pt, fb=fb: e.tensor_copy(ufn[:, fb, :], pt[:]), reads=[pb], writes=[ufn_b])
                elif fb < 6:
                    s, sbuf_b = next_st()
                    c.act.op(lambda e, pt=pt, s=s: e.activation(s[:], pt[:], AF.Copy, scale=0.125), reads=[pb], writes=[sbuf_b])
                    store(qTo[(fb - 2) * 128:(fb - 1) * 128, tsl], s[:], sbuf_b)
                elif fb < 10:
                    s, sbuf_b = next_st()
                    c.dve.op(lambda e, pt=pt, s=s: e.tensor_copy(s[:], pt[:]), reads=[pb], writes=[sbuf_b])
                    store(kTo[(fb - 6) * 128:(fb - 5) * 128, tsl], s[:], sbuf_b)
                elif fb < 16:
                    s, sbuf_b = next_st()
                    c.dve.op(lambda e, pt=pt, s=s: e.tensor_copy(
                        s[:].rearrange("p (t c) -> p t c", t=8), pt[:].rearrange("p (c t) -> p t c", t=8)),
                        reads=[pb], writes=[sbuf_b])
                    for g in range(8):
                        gg = (fb - 14) * 8 + g
                        store(Uo[gg].rearrange("(t ci) c -> ci t c", ci=16)[:, :, tt * 64:(tt + 1) * 64],
                              s[g * 16:(g + 1) * 16, :].rearrange("p (t c) -> p t c", t=8), sbuf_b)
                else:
                    s, sbuf_b = next_st()
                    c.act.op(lambda e, pt=pt, s=s: e.activation(s[:], pt[:], AF.Sigmoid), reads=[pb], writes=[sbuf_b])
                    store(gTo[(fb - 16) * 128:(fb - 15) * 128, tsl], s[:], sbuf_b)
            for sub in range(4):
                ssl = slice(sub * 128, (sub + 1) * 128)
                pt, pb = next_ps()
                for k in range(8):
                    c.pe.op(lambda e, k=k, pt=pt, ssl=ssl: e.matmul(pt[:], h[:, k, ssl], wbf[:, k, 1280:1792],
                                                                     start=(k == 0), stop=(k == 7)),
                            reads=[wbf_b, hb], writes=[pb])
                s, sbuf_b = next_st()
                c.dve.op(lambda e, pt=pt, s=s: e.tensor_copy(s[:], pt[:]), reads=[pb], writes=[sbuf_b])
                store(vo[tt * TT + sub * 128: tt * TT + (sub + 1) * 128, :], s[:], sbuf_b)
                pt, pb = next_ps()
                for j in range(2):
                    c.pe.op(lambda e, j=j, pt=pt, ssl=ssl: e.matmul(pt[:], ufn[:, j, ssl], wcb[:, j, :],
                                                                     start=(j == 0), stop=(j == 1)),
                            reads=[ufn_b, wcb_b], writes=[pb])
                s, sbuf_b = next_st()
                c.act.op(lambda e, pt=pt, s=s: e.activation(s[:], pt[:], AF.Copy), reads=[pb], writes=[sbuf_b])
                store(Vo[tt * TT + sub * 128: tt * TT + (sub + 1) * 128, :], s[:], sbuf_b)
        c.finish()
    return nc


def wc_const():
    w = np.zeros((256, 512), np.float32)
    cc = np.arange(64)
    ang = 2 * np.pi * np.outer(cc, cc) / 64
    for g in range(4):
        w[g * 64:(g + 1) * 64, g * 64:(g + 1) * 64] = np.cos(ang) / 8
        w[g * 64:(g + 1) * 64, 256 + g * 64:256 + (g + 1) * 64] = -np.sin(ang) / 8
    return w


D = 1024
DFF = 4096
EPS = 1e-6
GELU_MODE = ["tanh_manual"]


def load_cast(nc, c, es_tmp, name, dst, src_view, kparts, cols, scale=None, chunk=512):
    st = [T(es_tmp.enter_context(nc.sbuf_tensor(f"{name}_st{i}", [128, chunk], F32))[:]) for i in range(3)]
    i = 0
    for k in range(kparts):
        for c0 in range(0, cols, chunk):
            w = min(chunk, cols - c0)
            s = st[i % 3]
            c.sp.dma(s.ap[:, 0:w], src_view[:, k, c0:c0 + w], writes=[s.b])
            eng = c.dve if i % 2 == 0 else c.pool
            if scale is None:
                cp(eng, dst[:, k, c0:c0 + w], s[:, 0:w])
            else:
                ts(eng, dst[:, k, c0:c0 + w], s[:, 0:w], scale[:, k:k + 1], None, ALU.mult)
            i += 1


def emit_l3(nc, c, ps, xT, FfnT, OnaT, Ypre, gT, w_glu, w_br_fn, w_br_na, w_br_ssm, w_out, g_ffn, w_up, w_down,
            x1s, outT, g_final=None, NT=4096):
    TT = 512
    ntt = NT // TT
    xTv = xT.rearrange("k p n -> p k n")
    x1v = x1s.rearrange("k p n -> p k n")
    outv = outT.rearrange("k p n -> p k n")
    with contextlib.ExitStack() as es:
        def S(name, shape, dt):
            return T(es.enter_context(nc.sbuf_tensor(name, shape, dt))[:])
        wglu = S("wglu", [128, 2, 256], BF16)
        wfn = S("wfn", [128, 2, D], BF16)
        wna = S("wna", [128, 4, D], BF16)
        wss = S("wss", [128, 2, D], BF16)
        wout = S("wout", [128, 8, D], BF16)
        with contextlib.ExitStack() as est:
            load_cast(nc, c, est, "lg", wglu, w_glu.rearrange("(k p) c -> p k c", p=128), 2, 256)
            load_cast(nc, c, est, "lf", wfn, w_br_fn.rearrange("(k p) c -> p k c", p=128), 2, D)
            load_cast(nc, c, est, "ln", wna, w_br_na.rearrange("(k p) c -> p k c", p=128), 4, D)
            load_cast(nc, c, est, "ls", wss, w_br_ssm.rearrange("(k p) c -> p k c", p=128), 2, D)
            load_cast(nc, c, est, "lo", wout, w_out.rearrange("(k p) c -> p k c", p=128), 8, D)
        c.barrier()
        xt = [S(f"xt{i}", [128, 8, TT], F32) for i in range(2)]
        gt = [S(f"gt{i}", [128, 24, TT], BF16) for i in range(2)]
        fn = [S(f"fn{i}", [128, 2, TT], BF16) for i in range(2)]
        na = [S(f"na{i}", [128, 4, TT], BF16) for i in range(2)]
        yp = [S(f"yp{i}", [128, 2, TT], BF16) for i in range(2)]
        yT = S("yT", [128, 2, TT], BF16)
        yf = S("yf", [128, 2, TT], F32)
        yt1 = S("yt1", [128, 2, TT], F32)
        yt2 = S("yt2", [128, 2, TT], F32)
        sg = S("sg", [128, 2, TT], F32)
        bs = S("bs", [128, 2, TT], BF16)
        m1 = [S(f"m1_{i}", [128, TT], F32) for i in range(2)]
        m2 = [S(f"m2_{i}", [128, TT], F32) for i in range(2)]
        m3 = [S(f"m3_{i}", [128, TT], F32) for i in range(2)]
        mg = S("mg", [128, 8, TT], BF16)
        x1 = [S(f"x1_{i}", [128, 8, TT], F32) for i in range(2)]
        for tt_ in range(ntt):
            i = tt_ % 2
            tsl = slice(tt_ * TT, (tt_ + 1) * TT)
            c.sp.dma(xt[i].ap, xTv[:, :, tsl], writes=[xt[i].b])
            c.sp.dma(gt[i].ap, gT.rearrange("(j p) n -> p j n", p=128)[:, :, tsl], writes=[gt[i].b])
            c.sp.dma(fn[i].ap, FfnT.rearrange("(j p) n -> p j n", p=128)[:, :, tsl], writes=[fn[i].b])
            c.sp.dma(na[i].ap, OnaT.rearrange("(j p) n -> p j n", p=128)[:, :, tsl], writes=[na[i].b])
            for g in range(16):
                c.sp.dma(yp[i].ap[(g % 8) * 16:(g % 8 + 1) * 16, g // 8, :].rearrange("p (t c) -> p t c", t=8),
                         Ypre[g].rearrange("(t co) c -> co t c", co=16)[:, :, tt_ * 64:(tt_ + 1) * 64],
                         writes=[yp[i].b])
            ypv = T(yp[i].ap.rearrange("p j (t c) -> p j t c", t=8), yp[i].b)

            def perm(t_):
                return T(t_.ap.rearrange("p j (c t) -> p j t c", t=8), t_.b)
            cp(c.dve, perm(yf), ypv)
            tt(c.pool, yt1, yf, yf, ALU.mult)
            ts(c.dve, yt1, yt1, 0.044715 * 0.7978845608028654, 0.7978845608028654, ALU.mult, ALU.add)
            tt(c.pool, yt1, yt1, yf, ALU.mult)
            act(c, yt2, yt1, AF.Tanh)
            ts(c.dve, yt2, yt2, 0.5, 0.5, ALU.mult, ALU.add)
            tt(c.dve, yT, yt2, yf, ALU.mult)
            for mo in range(2):
                bank = ps.next()
                for j in range(2):
                    mm(c, bank, wglu[:, j, mo * 128:(mo + 1) * 128], yT[:, j, :], j == 0, j == 1)
                act(c, sg[:, mo, :], bank, AF.Sigmoid)
            tt(c.pool, yt1, yt2, yf, ALU.mult)
            tt(c.dve, bs, yt1, sg, ALU.mult)
            for fo in range(8):
                fsl = slice(fo * 128, (fo + 1) * 128)
                A = ps.next()
                for j in range(2):
                    mm(c, A, wfn[:, j, fsl], fn[i][:, j, :], j == 0, j == 1)
                B = ps.next()
                for j in range(4):
                    mm(c, B, wna[:, j, fsl], na[i][:, j, :], j == 0, j == 3)
                C = ps.next()
                for j in range(2):
                    mm(c, C, wss[:, j, fsl], bs[:, j, :], j == 0, j == 1)
                q = fo % 2
                tt(c.dve, m1[q], A, gt[i][:, fo, :], ALU.mult)
                tt(c.dve, m2[q], B, gt[i][:, 8 + fo, :], ALU.mult)
                tt(c.dve, m3[q], C, gt[i][:, 16 + fo, :], ALU.mult)
                tt(c.pool, m1[q], m1[q], m2[q], ALU.add)
                tt(c.pool, mg[:, fo, :], m1[q], m3[q], ALU.add)
            for fo in range(8):
                fsl = slice(fo * 128, (fo + 1) * 128)
                bank = ps.next()
                for k in range(8):
                    mm(c, bank, wout[:, k, fsl], mg[:, k, :], k == 0, k == 7)
                tt(c.dve, x1[i][:, fo, :], bank, xt[i][:, fo, :], ALU.add)
            c.sp.dma(x1v[:, :, tsl], x1[i].ap, reads=[x1[i].b])
    c.barrier()
    TB = 256
    ntb = NT // TB
    with contextlib.ExitStack() as es:
        def S(name, shape, dt):
            return T(es.enter_context(nc.sbuf_tensor(name, shape, dt))[:])
        wup = S("wup", [128, 8, DFF], BF16)
        wdn = S("wdn", [128, 32, D], BF16)
        gf = S("gf", [128, 8], F32)
        ones = S("ones3", [128, 128], BF16)
        c.pool.op(lambda e: e.memset(ones.ap, 1.0), writes=[ones.b])
        c.sp.dma(gf.ap, g_ffn, writes=[gf.b])
        gfin = None
        if g_final is not None:
            gfin = S("gfin", [128, 8], F32)
            c.sp.dma(gfin.ap, g_final, writes=[gfin.b])
        with contextlib.ExitStack() as est:
            load_cast(nc, c, est, "lu", wup, w_up.rearrange("(k p) c -> p k c", p=128), 8, DFF, scale=gf, chunk=1024)
            load_cast(nc, c, est, "ld", wdn, w_down.rearrange("(k p) c -> p k c", p=128), 32, D, chunk=1024)
        c.barrier()
        xa = [S(f"xa{i}", [128, 8, TB], F32) for i in range(2)]
        xo = [S(f"xo{i}", [128, 8, TB], F32) for i in range(1)] * 2
        xsq = S("xsq3", [128, 8, TB], BF16)
        rstd = S("rstd3", [128, TB], F32)
        h2 = S("h2", [128, 8, TB], BF16)
        rl = [S(f"rl{i}", [128, 2, TB], F32) for i in range(2)]
        a = S("a3", [128, 32, TB], BF16)
        for tb in range(ntb):
            i = tb % 2
            tsl = slice(tb * TB, (tb + 1) * TB)
            c.sp.dma(xa[i].ap, x1v[:, :, tsl], writes=[xa[i].b])
            act(c, xsq, xa[i], AF.Square)
            bank = ps.next()
            for k in range(8):
                mm(c, bank[:, 0:TB], ones, xsq[:, k, :], k == 0, k == 7)
            act(c, rstd, bank[:, 0:TB], AF.Sqrt, scale=1.0 / D, bias=EPS)
            c.dve.op(lambda e: e.reciprocal(rstd.ap, rstd.ap), reads=[rstd.b], writes=[rstd.b])
            tt(c.dve, h2, xa[i], T(rstd.ap.unsqueeze(1).to_broadcast([128, 8, TB]), rstd.b), ALU.mult)
            for fu2 in range(16):
                bank = ps.next()
                bv = T(bank.ap.rearrange("p (j n) -> p j n", j=2), bank.b)
                for j in range(2):
                    fu = fu2 * 2 + j
                    for k in range(8):
                        mm(c, bv[:, j, :], wup[:, k, fu * 128:(fu + 1) * 128], h2[:, k, :], k == 0, k == 7)
                r = rl[fu2 % 2]
                act(c, r, bv, AF.Relu)
                tt(c.pool, a[:, fu2 * 2:fu2 * 2 + 2, :], r, r, ALU.mult)
            for fo2 in range(4):
                bank = ps.next()
                bv = T(bank.ap.rearrange("p (j n) -> p j n", j=2), bank.b)
                for j in range(2):
                    fo = fo2 * 2 + j
                    for fu in range(32):
                        mm(c, bv[:, j, :], wdn[:, fu, fo * 128:(fo + 1) * 128], a[:, fu, :], fu == 0, fu == 31)
                tt(c.dve, xo[i][:, fo2 * 2:fo2 * 2 + 2, :], bv, xa[i][:, fo2 * 2:fo2 * 2 + 2, :], ALU.add)
            if g_final is not None:
                act(c, xsq, xo[i], AF.Square)
                bank = ps.next()
                for k in range(8):
                    mm(c, bank[:, 0:TB], ones, xsq[:, k, :], k == 0, k == 7)
                act(c, rstd, bank[:, 0:TB], AF.Sqrt, scale=1.0 / D, bias=EPS)
                c.dve.op(lambda e: e.reciprocal(rstd.ap, rstd.ap), reads=[rstd.b], writes=[rstd.b])
                tt(c.dve, xo[i], xo[i], T(rstd.ap.unsqueeze(1).to_broadcast([128, 8, TB]), rstd.b), ALU.mult)
                tt(c.dve, xo[i], xo[i], T(gfin.ap.unsqueeze(2).to_broadcast([128, 8, TB]), gfin.b), ALU.mult)
            c.sp.dma(outv[:, :, tsl], xo[i].ap, reads=[xo[i].b])


def build_l3(final=False, NT=4096):
    nc = bass.Bass("TRN2", target_bir_lowering=False)
    di = lambda n, s, d: nc.dram_tensor(n, s, d, kind="ExternalInput").ap()
    xT = di("xT", [8, 128, NT], F32)
    FfnT = di("FfnT", [256, NT], BF16)
    OnaT = di("OnaT", [512, NT], BF16)
    Ypre = di("Ypre", [16, 128, NT // 8], BF16)
    gT = di("gT", [3072, NT], BF16)
    w_glu = di("w_glu", [256, 256], F32)
    w_br_fn = di("w_br_fn", [256, D], F32)
    w_br_na = di("w_br_na", [512, D], F32)
    w_br_ssm = di("w_br_ssm", [256, D], F32)
    w_out = di("w_out", [D, D], F32)
    g_ffn = di("g_ffn", [128, 8], F32)
    w_up = di("w_up", [D, DFF], F32)
    w_down = di("w_down", [DFF, D], F32)
    g_final = di("g_final", [128, 8], F32) if final else None
    x1s = nc.dram_tensor("x1s", [8, 128, NT], F32, kind="ExternalOutput").ap()
    outT = nc.dram_tensor("outT", [8, 128, NT], F32, kind="ExternalOutput").ap()
    c = Ctx(nc)
    with contextlib.ExitStack() as es:
        ps = PsumPool(nc, es)
        emit_l3(nc, c, ps, xT, FfnT, OnaT, Ypre, gT, w_glu, w_br_fn, w_br_na, w_br_ssm, w_out, g_ffn, w_up, w_down,
                x1s, outT, g_final, NT)
        c.finish()
    return nc


NT = 4096
TT = 512
NTT = NT // TT
D = 1024
DIN = 5120
DFF = 4096
EPS = 1e-6
S_LEN = 8192
XC = 2048
R1 = 2304
R2 = 1024
RG = [[0, 1], [2, 3], [4, 5], [6, 7]]


class X1:
    @staticmethod
    def reg(buf, idx, off, n):
        return buf[idx * R1 + off: idx * R1 + off + n, :]

    @staticmethod
    def Vh(buf, idx):
        return X1.reg(buf, idx, 0, 512).rearrange("a (b c) -> (a b) c", c=256)

    @staticmethod
    def qTh(buf, idx):
        return X1.reg(buf, idx, 512, 512).rearrange("(f t) c -> f (t c)", t=2)

    @staticmethod
    def kTh(buf, idx):
        return X1.reg(buf, idx, 1024, 512).rearrange("(f t) c -> f (t c)", t=2)

    @staticmethod
    def vh(buf, idx):
        return X1.reg(buf, idx, 1536, 512).rearrange("a (b c) -> (a b) c", c=256)

    @staticmethod
    def Uh(buf, idx):
        return X1.reg(buf, idx, 2048, 256).rearrange("(g r) c -> g (r c)", g=8).rearrange("g (p n) -> g p n", n=512)


class X2:
    @staticmethod
    def reg(buf, idx, off, n):
        return buf[idx * R2 + off: idx * R2 + off + n, :]

    @staticmethod
    def Ffn(buf, idx):
        return X2.reg(buf, idx, 0, 256).rearrange("(f t) c -> f (t c)", t=2)

    @staticmethod
    def Ona(buf, idx):
        return X2.reg(buf, idx, 256, 512).rearrange("(f t) c -> f (t c)", t=2)

    @staticmethod
    def Ypre(buf, idx):
        return X2.reg(buf, idx, 768, 256).rearrange("(g r) c -> g (r c)", g=8).rearrange("g (p n) -> g p n", n=512)


def precast_units(nc, c, es, jobs, chunk=2048):
    NS = 3
    stf = [T(es.enter_context(nc.sbuf_tensor(f"pc_f{i}", [128, chunk], F32))[:]) for i in range(NS)]
    stb = [T(es.enter_context(nc.sbuf_tensor(f"pc_b{i}", [128, chunk], BF16))[:]) for i in range(NS)]
    i = 0
    for src, dst in jobs:
        K, C = src.shape[1], src.shape[2]
        for k in range(K):
            for c0 in range(0, C, chunk):
                w = min(chunk, C - c0)
                a, b = stf[i % NS], stb[i % NS]
                c.sp.dma(a.ap[:, 0:w], src[:, k, c0:c0 + w], writes=[a.b])
                act(c, b[:, 0:w], a[:, 0:w], AF.Copy)
                c.sp.dma(dst[:, k, c0:c0 + w], b.ap[:, 0:w], reads=[b.b])
                i += 1
                yield i


def emit_l1f(nc, c, ps, xTv, w_in, g_mix, wc, msk, In1, gTd, wbf_src=None):
    class _SP:
        def dma(self_, *a, **k):
            if SKIP_L1_ST[0]:
                return None
            return c.sp.dma(*a, **k)
    spx = _SP()
    with contextlib.ExitStack() as es:
        def S(name, shape, dt):
            return T(es.enter_context(nc.sbuf_tensor(name, shape, dt))[:])
        wbf = S("a_wbf", [128, 8, DIN], BF16)
        gm = S("a_gm", [128, 8], F32)
        wcb = S("a_wcb", [128, 2, 512], BF16)
        ones = S("a_ones", [128, 128], BF16)
        c.sp.dma(gm.ap, g_mix, writes=[gm.b])
        c.pool.op(lambda e: e.memset(ones.ap, 1.0), writes=[ones.b])
        with contextlib.ExitStack() as est:
            wcf = T(est.enter_context(nc.sbuf_tensor("a_wcf", [128, 2, 512], F32))[:])
            c.sp.dma(wcf.ap, wc.rearrange("(j p) c -> p j c", p=128), writes=[wcf.b])
            cp(c.dve, wcb, wcf)
            if wbf_src is None:
                load_cast(nc, c, est, "a_lw", wbf, w_in.rearrange("(k p) c -> p k c", p=128), 8, DIN, scale=gm, chunk=1280)
            else:
                for k in range(8):
                    c.sp.dma(wbf.ap[:, k, :], wbf_src[:, k, :], writes=[wbf.b])
        c.barrier()
        xt = [S(f"a_xt{i}", [128, 8, TT], F32) for i in range(2)]
        xsq = S("a_xsq", [128, 8, TT], BF16)
        hT = [S(f"a_hT{i}", [128, 8, TT], BF16) for i in range(2)]
        rstd = S("a_rstd", [128, TT], F32)
        ufn = S("a_ufn", [128, 2, TT], BF16)
        NST = 8
        st = [S(f"a_st{i}", [128, TT], BF16) for i in range(NST)]
        sti = [0]

        def next_st():
            i = sti[0] % NST
            sti[0] += 1
            return st[i]

        def evac2(pt, perm=False, qscale=False):
            outs = []
            for s in range(2):
                o = next_st()
                col = (2 + s) if qscale else s
                if perm:
                    ts(c.dve, T(o.ap.rearrange("p (t c) -> p t c", t=8), o.b),
                       T(pt.ap.rearrange("p (c t) -> p t c", t=8), pt.b), msk[:, col:col + 1], None, ALU.mult)
                elif s == 0 and USE_ACT_SCALE[0]:
                    c.act.op(lambda e, o=o, pt=pt, col=col: e.activation(o.ap, pt.ap, AF.Copy, scale=msk.ap[:, col:col + 1]),
                             reads=[pt.b, msk.b], writes=[o.b])
                else:
                    ts(c.dve, o, pt, msk[:, col:col + 1], None, ALU.mult)
                outs.append(o)
            return outs

        def load_x(t_):
            c.sp.dma(xt[t_ % 2].ap, xTv[:, :, t_ * TT:(t_ + 1) * TT], writes=[xt[t_ % 2].b])

        def norm(t_):
            xi_ = t_ % 2
            act(c, xsq, xt[xi_], AF.Square)
            bank = ps.next()
            for k in range(8):
                mm(c, bank, ones, xsq[:, k, :], k == 0, k == 7)
            act(c, rstd, bank, AF.Sqrt, scale=1.0 / D, bias=EPS)
            c.dve.op(lambda e: e.reciprocal(rstd.ap, rstd.ap), reads=[rstd.b], writes=[rstd.b])
            for k in range(8):
                if wbf_src is None:
                    eng = c.dve if k % 2 == 0 else c.pool
                    tt(eng, hT[xi_][:, k, :], xt[xi_][:, k, :], rstd, ALU.mult)
                else:
                    stt(c.dve, hT[xi_][:, k, :], xt[xi_][:, k, :], gm[:, k:k + 1], rstd, ALU.mult, ALU.mult)

        load_x(0)
        if NTT > 1:
            load_x(1)
        norm(0)
        for tt_ in range(NTT):
            xi = tt_ % 2
            tsl = slice(tt_ * TT, (tt_ + 1) * TT)
            if tt_ + 1 < NTT:
                norm(tt_ + 1)
            if tt_ + 2 < NTT:
                load_x(tt_ + 2)
            h = hT[xi]
            for fb in range(40):
                if 10 <= fb < 14:
                    continue
                pt = ps.next()
                for k in range(8):
                    mm(c, pt, wbf[:, k, fb * 128:(fb + 1) * 128], h[:, k, :], k == 0, k == 7)
                if fb < 2:
                    cp(c.dve, ufn[:, fb, :], pt)
                elif fb < 10:
                    isq = fb < 6
                    f = (fb - 2) if isq else (fb - 6)
                    j, f0 = f // 2, (f % 2) * 128
                    o = evac2(pt, qscale=isq)
                    for s in range(2):
                        view = (X1.qTh if isq else X1.kTh)(In1, j * 2 + s)
                        spx.dma(view[f0:f0 + 128, tsl], o[s].ap, reads=[o[s].b])
                elif fb < 16:
                    j = fb - 14
                    o = evac2(pt, perm=True)
                    for s in range(2):
                        Uv = X1.Uh(In1, j * 2 + s)
                        for g in range(8):
                            spx.dma(Uv[g].rearrange("(t ci) n -> ci t n", ci=16)[:, :, tt_ * 64:(tt_ + 1) * 64],
                                     o[s].ap[g * 16:(g + 1) * 16, :].rearrange("p (t n) -> p t n", t=8), reads=[o[s].b])
                else:
                    o = next_st()
                    act(c, o, pt, AF.Sigmoid)
                    c.sp.dma(gTd[(fb - 16) * 128:(fb - 15) * 128, tsl], o.ap, reads=[o.b])
            for sub in range(4):
                ssl = slice(sub * 128, (sub + 1) * 128)
                rows = slice(tt_ * TT + sub * 128, tt_ * TT + (sub + 1) * 128)
                pt = ps.next()
                for k in range(8):
                    mm(c, pt, h[:, k, ssl], wbf[:, k, 1280:1792], k == 0, k == 7)
                o = evac2(pt)
                for s in range(2):
                    for j in range(2):
                        spx.dma(X1.vh(In1, j * 2 + s)[rows, :], o[s].ap[:, j * 256:(j + 1) * 256], reads=[o[s].b])
                pt = ps.next()
                for jj in range(2):
                    mm(c, pt, ufn[:, jj, ssl], wcb[:, jj, :], jj == 0, jj == 1)
                o = evac2(pt)
                for s in range(2):
                    for j in range(2):
                        spx.dma(X1.Vh(In1, j * 2 + s)[rows, :].rearrange("t (a c) -> t a c", a=2),
                                 o[s].ap.rearrange("p (a j c) -> p a j c", a=2, j=2)[:, :, j, :], reads=[o[s].b])


def emit_fourier_f(nc, c, ps, Out1, consts, msk, In2):
    with contextlib.ExitStack() as es:
        def S(name, shape, dt):
            return T(es.enter_context(nc.sbuf_tensor(name, shape, dt))[:])
        X = S("fX", [128, 64, 256], BF16)
        R1f = S("fR1f", [128, 256], F32)
        R2f = S("fR2f", [128, 256], F32)
        R1b = S("fR1b", [128, 256], BF16)
        R2b = S("fR2b", [128, 256], BF16)
        CC = S("fCCs", [64, 256], F32)
        SS = S("fSSs", [64, 256], F32)
        F3f = S("fF3f", [64, 128], F32)
        F3 = S("fF3b", [64, 128], BF16)
        A2 = S("fA2", [64, 128, 256], BF16)
        A1s = [S(f"fA1s{i}", [64, 2, 256], F32) for i in range(2)]
        t1 = [S(f"ft1{i}", [64, 2, 256], F32) for i in range(2)]
        t2 = [S(f"ft2{i}", [64, 2, 256], F32) for i in range(2)]
        OT = [S(f"fOT{i}", [128, S_LEN], BF16) for i in range(2)]
        OU = S("fOU", [128, S_LEN], BF16)
        for s in range(2):
            c.sp.dma(X.ap[s * 64:(s + 1) * 64], X1.Vh(Out1, s).rearrange("(s1 s2) c -> s1 s2 c", s2=64), writes=[X.b])
        c.sp.dma(R1f.ap, consts["fR1"], writes=[R1f.b])
        c.sp.dma(R2f.ap, consts["fR2"], writes=[R2f.b])
        c.sp.dma(CC.ap, consts["fCC"], writes=[CC.b])
        c.sp.dma(SS.ap, consts["fSS"], writes=[SS.b])
        c.sp.dma(F3f.ap, consts["fF3"], writes=[F3f.b])
        cp(c.dve, R1b, R1f)
        cp(c.dve, R2b, R2f)
        cp(c.dve, F3, F3f)
        CCb = T(CC.ap.unsqueeze(1).to_broadcast([64, 2, 256]), CC.b)
        SSb = T(SS.ap.unsqueeze(1).to_broadcast([64, 2, 256]), SS.b)
        for fp in range(64):
            bank = ps.next()
            bv = T(bank.ap[0:64, :].rearrange("p (f c) -> p f c", f=2), bank.b)
            for j in range(2):
                f = fp * 2 + j
                mm(c, bv[:, j, :], X[:, :, f], R1b, True, False)
                mm(c, bv[:, j, :], X[:, :, 128 + f], R2b, False, True)
            i = fp % 2
            act(c, A1s[i], bv, AF.Copy)
            tt(c.dve, t1[i], A1s[i], CCb, ALU.mult)
            tt(c.pool, t2[i][:, :, 0:128], A1s[i][:, :, 128:256], SSb[:, :, 0:128], ALU.mult)
            tt(c.pool, t2[i][:, :, 128:256], A1s[i][:, :, 0:128], SSb[:, :, 128:256], ALU.mult)
            tt(c.dve, A2[:, fp * 2:fp * 2 + 2, :], t1[i], t2[i], ALU.add)
        OUv = OU.re("p (k2 k1) -> p k2 k1", k1=128)
        for kb in range(16):
            bank = ps.next()
            bv = T(bank.ap.rearrange("p (a k2) -> p a k2", a=8), bank.b)
            for a in range(8):
                k1 = kb * 8 + a
                mm(c, bv[:, a, :], A2[:, :, k1], F3[:, 0:64], True, False)
                mm(c, bv[:, a, :], A2[:, :, 128 + k1], F3[:, 64:128], False, True)
            src = T(bank.ap.rearrange("p (a k2) -> p k2 a", a=8), bank.b)
            if kb % 2 == 0:
                cp(c.dve, OUv[:, :, kb * 8:(kb + 1) * 8], src)
            else:
                act(c, OUv[:, :, kb * 8:(kb + 1) * 8], src, AF.Copy)
        for hh_ in range(4):
            hsl = slice(hh_ * 2048, (hh_ + 1) * 2048)
            ts(c.dve, OT[0][:, hsl], OU[:, hsl], msk[:, 0:1], None, ALU.mult)
            ts(c.pool, OT[1][:, hsl], OU[:, hsl], msk[:, 1:2], None, ALU.mult)
        for r in range(2):
            for s in range(2):
                c.sp.dma(X2.Ffn(In2, r * 2 + s), OT[s].ap[:, r * 4096:(r + 1) * 4096], reads=[OT[s].b])


def emit_na_f(nc, c, psS, psV, Out1, biasT, maskc, msk, In2, bg_jobs=None):
    nrows = 128
    with contextlib.ExitStack() as es:
        def S(name, shape, dt):
            return T(es.enter_context(nc.sbuf_tensor(name, shape, dt))[:])
        MM = S("nMM", [128, 4, 14, 64], F32)
        mk = S("nmsk", [128, 64], F32)
        ones = S("nones", [128, 64], BF16)
        qT = S("nqT", [128, S_LEN], BF16)
        kT = S("nkT", [128, S_LEN], BF16)
        vE = S("nvE", [128, 64, 128], BF16)
        vO = S("nvO", [128, 64, 128], BF16)
        Ot = [[S(f"nO{i}_{s}", [64, S_LEN], BF16) for s in range(2)] for i in range(2)]
        NBUF = 6
        sb = [S(f"nsb{i}", [128, 4, 64], F32) for i in range(NBUF)]
        ex = [S(f"nex{i}", [128, 4, 64], BF16) for i in range(NBUF)]
        rB = [S(f"nrB{i}", [64, 4, 64], F32) for i in range(2)]
        on = [S(f"non{i}", [64, 4, 64], F32) for i in range(2)]
        c.pool.op(lambda e: e.memset(ones.ap, 1.0), writes=[ones.b])
        c.sp.dma(mk.ap, maskc, writes=[mk.b])
        for h in range(4):
            c.sp.dma(MM.ap[0:64, h, :, :], biasT[h, 0:14].rearrange("dr kc qc -> kc dr qc"), writes=[MM.b])
            c.sp.dma(MM.ap[64:128, h, :, :], biasT[h, 1:15].rearrange("dr kc qc -> kc dr qc"), writes=[MM.b])
        MMf = MM.re("p h d q -> p (h d) q")
        tt(c.dve, MMf, MMf, T(mk.ap.unsqueeze(1).to_broadcast([128, 56, 64]), mk.b), ALU.add)
        it = 0
        bg = precast_units(nc, c, es, bg_jobs) if bg_jobs else None
        for p in range(2):
            for s in range(2):
                hs = slice(s * 4096, (s + 1) * 4096)
                c.sp.dma(qT.ap[:, hs], X1.qTh(Out1, s)[p * 128:(p + 1) * 128, :], writes=[qT.b])
                c.sp.dma(kT.ap[:, hs], X1.kTh(Out1, s)[p * 128:(p + 1) * 128, :], writes=[kT.b])
                vsrc = X1.vh(Out1, s)[:, p * 128:(p + 1) * 128]
                c.sp.dma(vE.ap[:, s * 32:(s + 1) * 32, :], vsrc.rearrange("(j p) c -> p j c", p=128), writes=[vE.b])
                c.sp.dma(vO.ap[:, s * 32:s * 32 + 31, :], vsrc[64:64 + 31 * 128, :].rearrange("(j p) c -> p j c", p=128),
                         writes=[vO.b])
            v0 = X1.vh(Out1, 0)[:, p * 128:(p + 1) * 128]
            v1 = X1.vh(Out1, 1)[:, p * 128:(p + 1) * 128]
            c.sp.dma(vO.ap[0:64, 31, :], v0[4032:4096, :], writes=[vO.b])
            c.sp.dma(vO.ap[64:128, 31, :], v1[0:64, :], writes=[vO.b])
            for hp in range(2):
                hh = p * 2 + hp
                O = Ot[hp]
                psl = slice(hp * 64, (hp + 1) * 64)
                pvs = {}
                exs = {}
                DEPTH = 3

                def scores(r, hh=hh, psl=psl):
                    nonlocal it
                    start = min(max(r - 4, 0), 120)
                    dr0 = start - r + 7
                    sbank = psS.next()
                    sv = T(sbank.ap[:, 0:256].rearrange("p (k q) -> p k q", k=4), sbank.b)
                    for kt in range(4):
                        k0 = (start + 2 * kt) * 64
                        mm(c, sv[:, kt, :], kT[psl, k0:k0 + 128], qT[psl, r * 64:(r + 1) * 64], True, True)
                    i = it % NBUF
                    it += 1
                    tt(c.dve, sb[i], sv, MM[:, hh, dr0:dr0 + 7:2, :], ALU.add)
                    act(c, ex[i], sb[i], AF.Exp)
                    exs[r] = ex[i]

                def pvstep(r, O=O, psl=psl):
                    start = min(max(r - 4, 0), 120)
                    rr = r % 4
                    if rr == 0:
                        pvb = psV.next()
                        pvs[r // 4] = T(pvb.ap[0:64, :].rearrange("p (r a q) -> p r a q", r=4, a=2), pvb.b)
                    pv = pvs[r // 4]
                    e_ = exs.pop(r)
                    vX = vE if start % 2 == 0 else vO
                    for kt in range(4):
                        j = (start + 2 * kt) // 2
                        mm(c, pv[:, rr, 0, :], vX[:, j, psl], e_[:, kt, :], kt == 0, kt == 3)
                    for kt in range(4):
                        mm(c, pv[:, rr, 1, :], ones, e_[:, kt, :], kt == 0, kt == 3)
                    if rr == 3:
                        q = (r // 4) % 2
                        rb = rB[q]
                        c.dve.op(lambda e, rb=rb, pv=pv: e.reciprocal(rb.ap, pv.ap[:, :, 1, :]), reads=[pv.b], writes=[rb.b])
                        tt(c.dve, on[q], pv[:, :, 0, :], rb, ALU.mult)
                        osl = slice((r - 3) * 64, (r + 1) * 64)
                        for s in range(2):
                            ts(c.pool, T(O[s].ap[:, osl].rearrange("p (r q) -> p r q", r=4), O[s].b), on[q],
                               msk[0:64, s:s + 1], None, ALU.mult)
                        del pvs[r // 4]

                for r in range(nrows + DEPTH):
                    if r < nrows:
                        scores(r)
                    if r >= DEPTH:
                        pvstep(r - DEPTH)
                    if bg is not None and r % 6 == 3:
                        if next(bg, None) is None:
                            bg = None
                for r_ in range(2):
                    for s in range(2):
                        c.sp.dma(X2.Ona(In2, r_ * 2 + s)[hh * 64:(hh + 1) * 64, :], O[s].ap[:, r_ * 4096:(r_ + 1) * 4096],
                                 reads=[O[s].b])
        if bg is not None:
            for _ in bg:
                pass


def emit_s5_f(nc, c, ps, Out1, pA, pB, pS, cE, dskd, msk, In2, cId, sync=None):
    nch = NCH
    FS = [128, 8, 128]
    with contextlib.ExitStack() as es:
        def S(name, shape, dt):
            return T(es.enter_context(nc.sbuf_tensor(name, shape, dt))[:])
        WsR = S("sWsR", FS, BF16)
        WsI = S("sWsI", FS, BF16)
        Kg = S("sKg", FS, BF16)
        Or = S("sOr", FS, F32)
        nOi = S("snOi", FS, F32)
        MR = S("sMR", [128, 8, 2], F32)
        MI = S("sMI", [128, 8], F32)
        nMI = S("snMI", [128, 8], F32)
        RHO = S("sRHO", [128, 8], F32)
        PHI = S("sPHI", [128, 8], F32)
        dsk = S("sdsk", [128, 8], F32)
        U = S("sU", [128, 8, nch], BF16)
        XSt = es.enter_context(nc.sbuf_tensor("sXS", [128, 8, 2, nch + 2], F32))[:]
        XSf = T(XSt[0:64], Buf("XSf"))
        XSb = T(XSt[64:128], Buf("XSb"))
        Yst = [S(f"sY{i}", [128, 512], F32) for i in range(2)]
        Ys2 = [[S(f"sYs{i}_{s}", [128, 512], BF16) for s in range(2)] for i in range(2)]
        cEt = S("scE", [128, 6, 128], F32)
        c.sp.dma(dsk.ap, dskd, writes=[dsk.b])
        c.sp.dma(cEt.ap, cE.rearrange("e p n -> p e n"), writes=[cEt.b])

        def Eb(i):
            return T(cEt.ap[:, i, :].unsqueeze(1).to_broadcast(FS), cEt.b)
        with contextlib.ExitStack() as esA:
            sc = Scratch(nc, esA, "sA", 14, FS)
            sci = T(esA.enter_context(nc.sbuf_tensor("sAi", FS, I32))[:])
            pAt = T(esA.enter_context(nc.sbuf_tensor("spA", [128, 5, 8, 128], F32))[:])
            c.sp.dma(pAt.ap, pA.rearrange("e p g n -> p e g n"), writes=[pAt.b])
            are, aim, ldt, bre, bim = [pAt[:, i] for i in range(5)]
            dt = sc.get()
            ar = sc.get()
            ai = sc.get()
            act(c, dt, ldt, AF.Exp)
            tt(c.dve, ar, are, dt, ALU.mult)
            tt(c.dve, ai, aim, dt, ALU.mult)
            lbr = sc.get()
            lbi = sc.get()
            cpow(c, None, ar, ai, lbr, lbi, sc, sci)
            cr = sc.get()
            ci = sc.get()
            coef(c, lbr, lbi, are, aim, cr, ci, sc)
            cmul(c, c.dve, lbr, lbi, cr, ci, bre, bim, sc)
            cpow(c, Eb(0), ar, ai, cr, ci, sc, sci)
            cmul(c, c.dve, WsR, WsI, cr, ci, lbr, lbi, sc)
        c.barrier()
        with contextlib.ExitStack() as esB:
            sc = Scratch(nc, esB, "sB", 12, FS)
            scs = Scratch(nc, esB, "sBs", 14, [128, 8])
            sci = T(esB.enter_context(nc.sbuf_tensor("sBi", FS, I32))[:])
            scis = T(esB.enter_context(nc.sbuf_tensor("sBis", [128, 8], I32))[:])
            pBt = T(esB.enter_context(nc.sbuf_tensor("spB", [128, 4, 8, 128], F32))[:])
            pSt = T(esB.enter_context(nc.sbuf_tensor("spS", [128, 3, 8], F32))[:])
            c.sp.dma(pBt.ap, pB.rearrange("e p g n -> p e g n"), writes=[pBt.b])
            c.sp.dma(pSt.ap, pS.rearrange("e p g -> p e g"), writes=[pSt.b])
            bre, bim, cre, cim = [pBt[:, i] for i in range(4)]
            sare, saim, sldt = [pSt[:, i] for i in range(3)]
            dts = scs.get()
            ars = scs.get()
            ais = scs.get()
            act(c, dts, sldt, AF.Exp)
            tt(c.dve, ars, sare, dts, ALU.mult)
            tt(c.dve, ais, saim, dts, ALU.mult)
            lbr = scs.get()
            lbi = scs.get()
            cpow(c, None, ars, ais, lbr, lbi, scs, scis)
            crs = scs.get()
            cis = scs.get()
            coef(c, lbr, lbi, sare, saim, crs, cis, scs)
            mur = scs.get()
            mui = scs.get()
            cpow(c, 8.0, ars, ais, mur, mui, scs, scis)
            cp(c.dve, MR[:, :, 0], mur)
            cp(c.dve, MR[:, :, 1], mur)
            cp(c.dve, MI, mui)
            ts(c.dve, nMI, mui, -1.0, None, ALU.mult)
            act(c, RHO, ars, AF.Exp, scale=8.0)
            ts(c.dve, mur, ais, 8.0, None, ALU.mult)
            phase_reduce(c, PHI, mur, mui, scis)

            def bc(t):
                return T(t.ap.unsqueeze(2).to_broadcast(FS), t.b)
            Bbr = sc.get()
            Bbi = sc.get()
            cmul(c, c.dve, Bbr, Bbi, bc(crs), bc(cis), bre, bim, sc)
            pr = sc.get()
            pi = sc.get()
            PTr = sc.get()
            PTi = sc.get()
            cpow(c, Eb(1), bc(ars), bc(ais), pr, pi, sc, sci)
            cmul(c, c.dve, PTr, PTi, pr, pi, Bbr, Bbi, sc)
            sc.put(Bbr, Bbi)
            QQr = sc.get()
            nQQi = sc.get()
            cpow(c, Eb(2), bc(ars), bc(ais), pr, pi, sc, sci)
            cmul(c, c.dve, QQr, nQQi, pr, pi, cre, cim, sc, neg_imag=True)
            cpow(c, Eb(3), bc(ars), bc(ais), pr, pi, sc, sci)
            cmul(c, c.dve, Or, nOi, pr, pi, cre, cim, sc, neg_imag=True)
            tmp1 = pr
            tmp2 = pi
            MF = T(cEt.ap[:, 4, :], cEt.b)
            MBk = T(cEt.ap[:, 5, :], cEt.b)
            Qd = [[sc.get(), sc.get()], [sc.get(), sc.get()]]
            for d in range(2):
                osl = slice((1 - d) * 64, (2 - d) * 64)
                ksl = slice(d * 64, (d + 1) * 64)
                for q, src in ((0, QQr), (1, nQQi)):
                    c.dve.op(lambda e, t=Qd[d][q], osl=osl: e.memset(t.ap[osl], 0.0), writes=[Qd[d][q].b])
                    cp(c.dve, Qd[d][q][ksl], src[ksl])
            for g in range(8):
                bank = ps.next()
                for d in range(2):
                    o = bank[:, d * 128:(d + 1) * 128]
                    mm(c, o, PTr[:, g, :], Qd[d][0][:, g, :], True, False)
                    mm(c, o, PTi[:, g, :], Qd[d][1][:, g, :], False, True)
                tt(c.dve, tmp1[:, g, :], bank[:, 0:128], MF, ALU.mult)
                tt(c.dve, tmp2[:, g, :], bank[:, 128:256], MBk, ALU.mult)
                tt(c.dve, Kg[:, g, :], tmp1[:, g, :], tmp2[:, g, :], ALU.add)
        c.barrier()
        if sync is not None:
            sync()
        for s in range(2):
            c.sp.dma(U.ap[:, :, s * 512:(s + 1) * 512], X1.Uh(Out1, s).rearrange("g p n -> p g n"), writes=[U.b])
        c.dve.op(lambda e: e.memset(XSt[0:64, :, :, 1:2], 0.0), writes=[XSf.b])
        c.dve.op(lambda e: e.memset(XSt[64:128, :, :, nch:nch + 1], 0.0), writes=[XSb.b])
        ncb = nch // 512
        for g in range(8):
            for cb in range(ncb):
                csl = slice(cb * 512, (cb + 1) * 512)
                bR = ps.next()
                mm(c, bR, WsR[:, g, :], U[:, g, csl], True, True)
                bI = ps.next()
                mm(c, bI, WsI[:, g, :], U[:, g, csl], True, True)
                fsl = slice(2 + cb * 512, 2 + (cb + 1) * 512)
                bsl = slice(cb * 512, (cb + 1) * 512)
                c.act.op(lambda e, bR=bR, g=g, fsl=fsl: e.activation(XSt[0:64, g, 0, fsl], bR.ap[0:64], AF.Copy),
                         reads=[bR.b], writes=[XSf.b])
                c.act.op(lambda e, bR=bR, g=g, bsl=bsl: e.activation(XSt[64:128, g, 0, bsl], bR.ap[64:128], AF.Copy),
                         reads=[bR.b], writes=[XSb.b])
                c.dve.op(lambda e, bI=bI, g=g, fsl=fsl: e.tensor_copy(XSt[0:64, g, 1, fsl], bI.ap[0:64]),
                         reads=[bI.b], writes=[XSf.b])
                c.dve.op(lambda e, bI=bI, g=g, bsl=bsl: e.tensor_copy(XSt[64:128, g, 1, bsl], bI.ap[64:128]),
                         reads=[bI.b], writes=[XSb.b])
        scan_hw(nc, c, es, XSt, XSf, XSb, RHO, PHI, cId, nch)
        it = 0
        for g in range(8):
            for cb in range(ncb):
                csl = slice(cb * 512, (cb + 1) * 512)
                yb = ps.next()
                mm(c, yb, Kg[:, g, :], U[:, g, csl], True, False)
                f0 = cb * 512
                xr = T(XSt[:, g, 0, f0 + 1:f0 + 513], XSf.b)
                xi = T(XSt[:, g, 1, f0 + 1:f0 + 513], XSb.b)
                c.pe.op(lambda e, yb=yb, g=g, xr=xr: e.matmul(yb.ap, Or.ap[:, g, :], xr.ap, start=False, stop=False),
                        reads=[Or.b, XSf.b, XSb.b], writes=[yb.b])
                c.pe.op(lambda e, yb=yb, g=g, xi=xi: e.matmul(yb.ap, nOi.ap[:, g, :], xi.ap, start=False, stop=True),
                        reads=[nOi.b, XSf.b, XSb.b], writes=[yb.b])
                q = it % 2
                it += 1
                y = Yst[q]
                stt(c.dve, y, U[:, g, csl], dsk[:, g:g + 1], yb, ALU.mult, ALU.add)
                for s in range(2):
                    eng = c.dve if s == 0 else c.pool
                    ts(eng, Ys2[q][s], y, msk[:, s:s + 1], None, ALU.mult)
                    c.sp.dma(X2.Ypre(In2, cb * 2 + s)[g], Ys2[q][s].ap, reads=[Ys2[q][s].b])


def emit_l3f(nc, c, ps, xinv, Out2, gTd, w_glu, w_br_fn, w_br_na, w_br_ssm, w_out, g_ffn, w_up, w_down,
             x1v, outv, g_final=None, sync=None, wbf_up=None, wbf_dn=None):
    ntt = NT // TT
    with contextlib.ExitStack() as es:
        def S(name, shape, dt):
            return T(es.enter_context(nc.sbuf_tensor(name, shape, dt))[:])
        wglu = S("wglu", [128, 2, 256], BF16)
        wfn = S("wfn", [128, 2, D], BF16)
        wna = S("wna", [128, 4, D], BF16)
        wss = S("wss", [128, 2, D], BF16)
        wout = S("wout", [128, 8, D], BF16)
        with contextlib.ExitStack() as est:
            load_cast(nc, c, est, "lg", wglu, w_glu.rearrange("(k p) c -> p k c", p=128), 2, 256)
            load_cast(nc, c, est, "lf", wfn, w_br_fn.rearrange("(k p) c -> p k c", p=128), 2, D)
            load_cast(nc, c, est, "ln", wna, w_br_na.rearrange("(k p) c -> p k c", p=128), 4, D)
            load_cast(nc, c, est, "ls", wss, w_br_ssm.rearrange("(k p) c -> p k c", p=128), 2, D)
            load_cast(nc, c, est, "lo", wout, w_out.rearrange("(k p) c -> p k c", p=128), 8, D)
        c.barrier()
        if sync is not None:
            sync()
        xt = [S(f"xt{i}", [128, 8, TT], F32) for i in range(2)]
        gt = [S(f"gt{i}", [128, 24, TT], BF16) for i in range(2)]
        fn = [S(f"fn{i}", [128, 2, TT], BF16) for i in range(2)]
        na = [S(f"na{i}", [128, 4, TT], BF16) for i in range(2)]
        yp = [S(f"yp{i}", [128, 2, TT], BF16) for i in range(2)]
        yT = S("yT", [128, 2, TT], BF16)
        yf = S("yf", [128, 2, TT], F32)
        yt1 = S("yt1", [128, 2, TT], F32)
        yt2 = S("yt2", [128, 2, TT], F32)
        sg = S("sg", [128, 2, TT], F32)
        bs = S("bs", [128, 2, TT], BF16)
        m1 = [S(f"m1_{i}", [128, TT], F32) for i in range(2)]
        m2 = [S(f"m2_{i}", [128, TT], F32) for i in range(2)]
        m3 = [S(f"m3_{i}", [128, TT], F32) for i in range(2)]
        mg = S("mg", [128, 8, TT], BF16)
        x1 = [S(f"x1_{i}", [128, 8, TT], F32) for i in range(2)]
        def loadsA(tt_):
            i = tt_ % 2
            tsl = slice(tt_ * TT, (tt_ + 1) * TT)
            c.sp.dma(xt[i].ap, xinv[:, :, tsl], writes=[xt[i].b])
            c.sp.dma(gt[i].ap, gTd.rearrange("(j p) n -> p j n", p=128)[:, :, tsl], writes=[gt[i].b])
            for j in range(2):
                c.sp.dma(fn[i].ap[:, j, :], X2.Ffn(Out2, j)[:, tsl], writes=[fn[i].b])
                c.sp.dma(na[i].ap[:, 2 * j:2 * j + 2, :], X2.Ona(Out2, j).rearrange("(k p) n -> p k n", p=128)[:, :, tsl],
                         writes=[na[i].b])
                Yv = X2.Ypre(Out2, j)
                for gl in range(8):
                    c.sp.dma(yp[i].ap[gl * 16:(gl + 1) * 16, j, :].rearrange("p (t n) -> p t n", t=8),
                             Yv[gl].rearrange("(t co) n -> co t n", co=16)[:, :, tt_ * 64:(tt_ + 1) * 64],
                             writes=[yp[i].b])

        loadsA(0)
        for tt_ in range(ntt):
            i = tt_ % 2
            tsl = slice(tt_ * TT, (tt_ + 1) * TT)
            if tt_ + 1 < ntt:
                loadsA(tt_ + 1)
            ypv = T(yp[i].ap.rearrange("p j (t c) -> p j t c", t=8), yp[i].b)

            def perm(t_):
                return T(t_.ap.rearrange("p j (c t) -> p j t c", t=8), t_.b)
            cp(c.dve, perm(yf), ypv)
            tt(c.pool, yt1, yf, yf, ALU.mult)
            ts(c.dve, yt1, yt1, 0.044715 * 0.7978845608028654, 0.7978845608028654, ALU.mult, ALU.add)
            tt(c.pool, yt1, yt1, yf, ALU.mult)
            act(c, yt2, yt1, AF.Tanh)
            ts(c.dve, yt2, yt2, 0.5, 0.5, ALU.mult, ALU.add)
            tt(c.dve, yT, yt2, yf, ALU.mult)
            for mo in range(2):
                bank = ps.next()
                for j in range(2):
                    mm(c, bank, wglu[:, j, mo * 128:(mo + 1) * 128], yT[:, j, :], j == 0, j == 1)
                act(c, sg[:, mo, :], bank, AF.Sigmoid)
            tt(c.pool, yt1, yt2, yf, ALU.mult)
            tt(c.dve, bs, yt1, sg, ALU.mult)
            for fo in range(8):
                fsl = slice(fo * 128, (fo + 1) * 128)
                A = ps.next()
                for j in range(2):
                    mm(c, A, wfn[:, j, fsl], fn[i][:, j, :], j == 0, j == 1)
                B = ps.next()
                for j in range(4):
                    mm(c, B, wna[:, j, fsl], na[i][:, j, :], j == 0, j == 3)
                C = ps.next()
                for j in range(2):
                    mm(c, C, wss[:, j, fsl], bs[:, j, :], j == 0, j == 1)
                q = fo % 2
                tt(c.dve, m1[q], A, gt[i][:, fo, :], ALU.mult)
                tt(c.dve, m2[q], B, gt[i][:, 8 + fo, :], ALU.mult)
                tt(c.dve, m3[q], C, gt[i][:, 16 + fo, :], ALU.mult)
                tt(c.pool, m1[q], m1[q], m2[q], ALU.add)
                tt(c.pool, mg[:, fo, :], m1[q], m3[q], ALU.add)
            for fo in range(8):
                fsl = slice(fo * 128, (fo + 1) * 128)
                bank = ps.next()
                for k in range(8):
                    mm(c, bank, wout[:, k, fsl], mg[:, k, :], k == 0, k == 7)
                tt(c.dve, x1[i][:, fo, :], bank, xt[i][:, fo, :], ALU.add)
            c.sp.dma(x1v[:, :, tsl], x1[i].ap, reads=[x1[i].b])
    c.barrier()
    TB = 256
    ntb = NT // TB
    with contextlib.ExitStack() as es:
        def S(name, shape, dt):
            return T(es.enter_context(nc.sbuf_tensor(name, shape, dt))[:])
        wup = S("wup", [128, 8, DFF], BF16)
        wdn = S("wdn", [128, 32, D], BF16)
        gf = S("gf", [128, 8], F32)
        ones = S("ones3", [128, 128], BF16)
        c.pool.op(lambda e: e.memset(ones.ap, 1.0), writes=[ones.b])
        c.sp.dma(gf.ap, g_ffn, writes=[gf.b])
        gfin = None
        if g_final is not None:
            gfin = S("gfin", [128, 8], F32)
            c.sp.dma(gfin.ap, g_final, writes=[gfin.b])
        if wbf_up is None:
            with contextlib.ExitStack() as est:
                load_cast(nc, c, est, "lu", wup, w_up.rearrange("(k p) c -> p k c", p=128), 8, DFF, scale=gf, chunk=1024)
                load_cast(nc, c, est, "ld", wdn, w_down.rearrange("(k p) c -> p k c", p=128), 32, D, chunk=1024)
            c.barrier()
        else:
            for k in range(8):
                c.sp.dma(wup.ap[:, k, :], wbf_up[:, k, :], writes=[wup.b])
            for k in range(4):
                c.sp.dma(wdn.ap[:, k * 8:(k + 1) * 8, :], wbf_dn[:, k * 8:(k + 1) * 8, :], writes=[wdn.b])
        xa = [S(f"xa{i}", [128, 8, TB], F32) for i in range(2)]
        xo = S("xo0", [128, 8, TB], F32)
        xsq = S("xsq3", [128, 8, TB], BF16)
        rstd = S("rstd3", [128, TB], F32)
        h2 = S("h2", [128, 8, TB], BF16)
        h2b = S("h2b", [128, 8, TB], BF16)
        rl = [S(f"rl{i}", [128, 2, TB], F32) for i in range(2)]
        a = S("a3", [128, 32, TB], BF16)
        c.sp.dma(xa[0].ap, x1v[:, :, 0:TB], writes=[xa[0].b])

        def normB(tb_):
            i_ = tb_ % 2
            act(c, xsq, xa[i_], AF.Square)
            bank = ps.next()
            for k in range(8):
                mm(c, bank[:, 0:TB], ones, xsq[:, k, :], k == 0, k == 7)
            act(c, rstd, bank[:, 0:TB], AF.Sqrt, scale=1.0 / D, bias=EPS)
            c.dve.op(lambda e: e.reciprocal(rstd.ap, rstd.ap), reads=[rstd.b], writes=[rstd.b])
            if wbf_up is None:
                tt(c.dve, h2n[(tb_ + 1) % 2], xa[i_], T(rstd.ap.unsqueeze(1).to_broadcast([128, 8, TB]), rstd.b), ALU.mult)
            else:
                for k in range(8):
                    stt(c.dve, h2n[(tb_ + 1) % 2][:, k, :], xa[i_][:, k, :], gf[:, k:k + 1], rstd, ALU.mult, ALU.mult)

        h2n = [h2, h2b]
        normB(0)
        for tb in range(ntb):
            i = tb % 2
            tsl = slice(tb * TB, (tb + 1) * TB)
            if tb + 1 < ntb:
                c.sp.dma(xa[(tb + 1) % 2].ap, x1v[:, :, (tb + 1) * TB:(tb + 2) * TB], writes=[xa[(tb + 1) % 2].b])
            hcur = h2n[(tb + 1) % 2]
            for fu2 in range(16):
                bank = ps.next()
                bv = T(bank.ap.rearrange("p (j n) -> p j n", j=2), bank.b)
                for j in range(2):
                    fu = fu2 * 2 + j
                    for k in range(8):
                        mm(c, bv[:, j, :], wup[:, k, fu * 128:(fu + 1) * 128], hcur[:, k, :], k == 0, k == 7)
                r = rl[fu2 % 2]
                act(c, r, bv, AF.Relu)
                tt(c.pool, a[:, fu2 * 2:fu2 * 2 + 2, :], r, r, ALU.mult)
            if tb + 1 < ntb:
                normB(tb + 1)
            for fo2 in range(4):
                bank = ps.next()
                bv = T(bank.ap.rearrange("p (j n) -> p j n", j=2), bank.b)
                for j in range(2):
                    fo = fo2 * 2 + j
                    for fu in range(32):
                        mm(c, bv[:, j, :], wdn[:, fu, fo * 128:(fo + 1) * 128], a[:, fu, :], fu == 0, fu == 31)
                tt(c.dve, xo[:, fo2 * 2:fo2 * 2 + 2, :], bv, xa[i][:, fo2 * 2:fo2 * 2 + 2, :], ALU.add)
            if g_final is not None:
                act(c, xsq, xo, AF.Square)
                bank = ps.next()
                for k in range(8):
                    mm(c, bank[:, 0:TB], ones, xsq[:, k, :], k == 0, k == 7)
                act(c, rstd, bank[:, 0:TB], AF.Sqrt, scale=1.0 / D, bias=EPS)
                c.dve.op(lambda e: e.reciprocal(rstd.ap, rstd.ap), reads=[rstd.b], writes=[rstd.b])
                tt(c.dve, xo, xo, T(rstd.ap.unsqueeze(1).to_broadcast([128, 8, TB]), rstd.b), ALU.mult)
                tt(c.dve, xo, xo, T(gfin.ap.unsqueeze(2).to_broadcast([128, 8, TB]), gfin.b), ALU.mult)
            c.sp.dma(outv[:, :, tsl], xo.ap, reads=[xo.b])
    c.barrier()


class NCProxy:
    def __init__(self, nc):
        self._nc = nc
        self._n = 0

    def __getattr__(self, k):
        return getattr(self._nc, k)

    def sbuf_tensor(self, name, shape, dt):
        self._n += 1
        return self._nc.sbuf_tensor(f"{name}_{self._n}", shape, dt)


def build_fused(depth):
    nc_real = bass.Bass("TRN2", target_bir_lowering=False)
    nc = NCProxy(nc_real)
    L = depth

    def di(n, s, d=F32):
        return nc.dram_tensor(n, s, d, kind="ExternalInput").ap()
    xT = di("xT", [8, 128, NT])
    mskd = di("msk", [128, 4])
    w_in = di("w_in", [L, D, DIN])
    g_mix = di("g_mix", [L, 128, 8])
    wc = di("wc", [256, 512])
    fc = {k: di(k, list(v.shape)) for k, v in fourier_consts().items()}
    biasT = di("biasT", [L, 4, 15, 64, 64])
    maskc = di("maskc", [128, 64])
    pA = di("pA", [L, 5, 128, 8, 128])
    pB = di("pB", [L, 4, 128, 8, 128])
    pS = di("pS", [L, 3, 128, 8])
    cE = di("cE", [6, 128, 128])
    cI = di("cI", [128, NCH])
    dskd = di("dsk", [L, 128, 8])
    w_glu = di("w_glu", [L, 256, 256])
    w_br_fn = di("w_br_fn", [L, 256, D])
    w_br_na = di("w_br_na", [L, 512, D])
    w_br_ssm = di("w_br_ssm", [L, 256, D])
    w_out = di("w_out", [L, D, D])
    g_ffn = di("g_ffn", [L, 128, 8])
    w_up = di("w_up", [L, D, DFF])
    w_down = di("w_down", [L, DFF, D])
    g_final = di("g_final", [128, 8])
    outT = nc.dram_tensor("outT", [8, 128, NT], F32, kind="ExternalOutput").ap()
    In1 = nc.dram_tensor("In1", [4 * R1, XC], BF16).ap()
    Out1 = nc.dram_tensor("Out1", [2 * R1, XC], BF16).ap()
    In2 = nc.dram_tensor("In2", [4 * R2, XC], BF16).ap()
    Out2 = nc.dram_tensor("Out2", [2 * R2, XC], BF16).ap()
    gTd = nc.dram_tensor("gTd", [3072, NT], BF16).ap()
    x1s = nc.dram_tensor("x1s", [8, 128, NT], F32).ap()
    wb_in = nc.dram_tensor("wb_in", [128, 8, DIN], BF16).ap()
    wb_up = nc.dram_tensor("wb_up", [128, 8, DFF], BF16).ap()
    wb_dn = nc.dram_tensor("wb_dn", [128, 32, D], BF16).ap()
    xbuf = [nc.dram_tensor(f"xbuf{i}", [8, 128, NT], F32).ap() for i in range(2)]
    c = Ctx(nc)
    with contextlib.ExitStack() as es:
        ps = PsumPool(nc, es)
        psS = PsumPool.__new__(PsumPool)
        psS.banks = ps.banks[0:6]
        psS.i = 0
        psV = PsumPool.__new__(PsumPool)
        psV.banks = ps.banks[6:8]
        psV.i = 0
        msk = T(es.enter_context(nc.sbuf_tensor("msk_sb", [128, 4], F32))[:])
        c.sp.dma(msk.ap, mskd, writes=[msk.b])
        xin = xT.rearrange("k p n -> p k n")
        for l in range(L):
            emit_l1f(nc, c, ps, xin, w_in[l], g_mix[l], wc, msk, In1, gTd, wbf_src=(wb_in if (l > 0 and PRECAST[0]) else None))
            if STOPF[0] < 1:
                break
            c.collective("ReduceScatter", ALU.add, RG, In1, Out1, wait=False)
            if STOPF[0] < 2:
                break
            emit_s5_f(nc, c, ps, Out1, pA[l], pB[l], pS[l], cE, dskd[l], msk, In2, cI, sync=c.collective_wait)
            c.barrier()
            if STOPF[0] < 3:
                break
            emit_fourier_f(nc, c, ps, Out1, fc, msk, In2)
            c.barrier()
            if STOPF[0] < 4:
                break
            jobs = None
            if PRECAST[0]:
                jobs = [(w_up[l].rearrange("(k p) c -> p k c", p=128), wb_up),
                        (w_down[l].rearrange("(k p) c -> p k c", p=128), wb_dn)]
                if l + 1 < L:
                    jobs.append((w_in[l + 1].rearrange("(k p) c -> p k c", p=128), wb_in))
            emit_na_f(nc, c, psS, psV, Out1, biasT[l], maskc, msk, In2, bg_jobs=jobs)
            if STOPF[0] < 5:
                break
            c.collective("ReduceScatter", ALU.add, RG, In2, Out2, wait=False)
            if STOPF[0] < 6:
                break
            last = (l == L - 1)
            xo = outT.rearrange("k p n -> p k n") if last else xbuf[l % 2].rearrange("k p n -> p k n")
            emit_l3f(nc, c, ps, xin, Out2, gTd, w_glu[l], w_br_fn[l], w_br_na[l], w_br_ssm[l], w_out[l], g_ffn[l],
                     w_up[l], w_down[l], x1s.rearrange("k p n -> p k n"), xo, g_final if last else None,
                     sync=c.collective_wait, wbf_up=(wb_up if PRECAST[0] else None), wbf_dn=(wb_dn if PRECAST[0] else None))
            xin = xo
        c.finish()
    return nc_real


_FUSED = {}
PRECAST = [True]
SKIP_L1_ST = [False]
USE_ACT_SCALE = [False]
STOPF = [99]


def kernel(x, g_mix, w_in, na_rpb, ssm_a_re, ssm_a_im, ssm_log_dt, ssm_b_re, ssm_b_im,
           ssm_c_re, ssm_c_im, ssm_d, w_glu, w_br_fn, w_br_na, w_br_ssm, w_out,
           g_ffn, w_up, w_down, g_final):
    from concourse.bass_utils import run_bass_kernel_spmd
    f32 = lambda a: np.ascontiguousarray(np.asarray(a, dtype=np.float32))
    x = f32(x)
    B, S, Dm = x.shape
    L = int(np.asarray(w_in).shape[0])
    if L not in _FUSED:
        _FUSED[L] = build_fused(L)
    nc = _FUSED[L]

    def gl(g):
        return np.ascontiguousarray(f32(g).reshape(-1, 8, 128).transpose(0, 2, 1))
    shared = {"w_in": f32(w_in), "g_mix": gl(g_mix), "wc": wc_const(), "maskc": na_mask_const(), "cE": s5_consts(), "cI": s5_cidx(),
              "w_glu": f32(w_glu), "w_br_fn": f32(w_br_fn), "w_br_na": f32(w_br_na), "w_br_ssm": f32(w_br_ssm),
              "w_out": f32(w_out), "g_ffn": gl(g_ffn), "w_up": f32(w_up), "w_down": f32(w_down),
              "g_final": gl(g_final)[0]}
    shared.update(fourier_consts())
    rpb = f32(na_rpb)
    per_half = []
    for half in range(2):
        pAs, pBs, pSs, dks = [], [], [], []
        for l in range(L):
            a, b_, s_, d_ = s5_param_layout(f32(ssm_a_re[l]), f32(ssm_a_im[l]), f32(ssm_log_dt[l]), f32(ssm_b_re[l]),
                                            f32(ssm_b_im[l]), f32(ssm_c_re[l]), f32(ssm_c_im[l]), f32(ssm_d[l]),
                                            range(half * 8, half * 8 + 8))
            pAs.append(a)
            pBs.append(b_)
            pSs.append(s_)
            dks.append(d_)
        m = np.zeros((128, 4), np.float32)
        m[:, half] = 1.0
        m[:, 2 + half] = 0.125
        per_half.append({"pA": np.stack(pAs), "pB": np.stack(pBs), "pS": np.stack(pSs), "dsk": np.stack(dks),
                         "biasT": np.stack([na_bias_layout(rpb[l][half * 4:(half + 1) * 4]) for l in range(L)]),
                         "msk": m})
    ims = []
    for cidx in range(8):
        b, hf = cidx // 2, cidx % 2
        im = {"xT": np.ascontiguousarray(x[b, hf * 4096:(hf + 1) * 4096].T).reshape(8, 128, 4096)}
        im.update(shared)
        im.update(per_half[hf])
        ims.append(im)
    res = run_bass_kernel_spmd(nc, ims, core_ids=list(range(8))).results
    out = np.zeros((B, S, Dm), np.float32)
    for cidx in range(8):
        if res[cidx] is None:
            continue
        b, hf = cidx // 2, cidx % 2
        out[b, hf * 4096:(hf + 1) * 4096] = np.asarray(res[cidx]["outT"]).reshape(1024, 4096).T
    return out
```
